# Optimizing a Trainium2 kernel written in Bass

```python
import math
import jax, jax.numpy as jnp
from jax import lax
import numpy as np

D_MODEL = 1024
BATCH = 8
SEQ = 4096
DEPTH = 2

CTX_LEN = 256
GRID_W = 64
N_EVEN = (DEPTH + 1) // 2
N_ODD = DEPTH // 2
EPS = 1e-6

HY_WIDTH = D_MODEL // 2
HY_SHORT_K = 3
HY_EMB_DIM = 33
HY_FILTER_HIDDEN = 64
HY_DECAY_FAST = 0.3
HY_DECAY_SLOW = 1.5
HY_DECAY_TARGET = 1e-2

NA_HEADS = 8
NA_HEAD_DIM = 64
NA_WIDTH = NA_HEADS * NA_HEAD_DIM
NA_WIN_ROWS = 8
NA_WIN_COLS = 16
NA_QBLOCK_COLS = 16
NA_KBLOCK_COLS = 32

CONF_WIDTH = D_MODEL
CONF_K = 31

D_FF = 2816
FFN_CONV_K = 3

NEG_BIG = -1e30

kernel_name = 'hybrid_hyena_natten_conformer_dit'


def rms_norm(t, g):
    tf = t.astype(jnp.float32)
    y = tf * lax.rsqrt(jnp.mean(tf * tf, axis=-1, keepdims=True) + EPS)
    return (y * g.astype(jnp.float32)).astype(t.dtype)


def layer_norm(t, g, b):
    tf = t.astype(jnp.float32)
    mu = jnp.mean(tf, axis=-1, keepdims=True)
    var = jnp.mean(jnp.square(tf - mu), axis=-1, keepdims=True)
    y = (tf - mu) * lax.rsqrt(var + EPS)
    return (y * g.astype(jnp.float32) + b.astype(jnp.float32)).astype(t.dtype)


def dwconv(t, w, b):
    k, ch = w.shape
    y = lax.conv_general_dilated(
        t, w[:, None, :].astype(t.dtype), window_strides=(1,),
        padding=[(k // 2, k // 2)], dimension_numbers=('NWC', 'WIO', 'NWC'),
        feature_group_count=ch)
    return y + b.astype(t.dtype)


def adaln_params(cond, w_mod, b_mod):
    m = jax.nn.silu(cond) @ w_mod + b_mod
    return jnp.split(m[:, None, :], 6, axis=-1)


def hyena_filters(length, w1, b1, w2, b2, w3, b3, w4, freq):
    bands = (HY_EMB_DIM - 1) // 2
    t01 = jnp.linspace(0.0, 1.0, length, dtype=jnp.float32)[:, None]
    w_pos = 2.0 * math.pi * jnp.arange(length, dtype=jnp.float32) / length
    f = jnp.linspace(1e-4, bands - 1, bands, dtype=jnp.float32)
    ang = w_pos[:, None] * f[None, :]
    z = jnp.concatenate([t01, jnp.cos(ang), -jnp.sin(ang)], axis=-1).astype(w1.dtype)
    hdn = jnp.sin(freq[0] * (z @ w1 + b1))
    hdn = jnp.sin(freq[1] * (hdn @ w2 + b2))
    hdn = jnp.sin(freq[2] * (hdn @ w3 + b3))
    filt = (hdn @ w4).astype(jnp.float32)
    deltas = jnp.abs(jnp.linspace(math.log(HY_DECAY_TARGET) / HY_DECAY_FAST,
                                  math.log(HY_DECAY_TARGET) / HY_DECAY_SLOW,
                                  HY_WIDTH, dtype=jnp.float32))
    decay = jnp.exp(-t01 * deltas[None, :])
    return filt[:, :HY_WIDTH] * decay, filt[:, HY_WIDTH:] * decay


def bidirectional_fft_conv(v, h_fwd, h_bwd, bias):
    length = v.shape[1]
    n = 2 * length
    k_full = jnp.concatenate([h_fwd, jnp.zeros_like(h_fwd[:1]), h_bwd[:0:-1]], axis=0)
    v_f = jnp.fft.rfft(v.astype(jnp.float32), n=n, axis=1)
    k_f = jnp.fft.rfft(k_full, n=n, axis=0)
    y = jnp.fft.irfft(v_f * k_f[None], n=n, axis=1)[:, :length]
    return (y + v.astype(jnp.float32) * bias.astype(jnp.float32)).astype(v.dtype)


def hyena_mixer(u, short_w, short_b, w1, b1, w2, b2, w3, b3, w4, freq, bias):
    uc = dwconv(u, short_w, short_b)
    x0, x1, v = jnp.split(uc, 3, axis=-1)
    h_fwd, h_bwd = hyena_filters(u.shape[1], w1, b1, w2, b2, w3, b3, w4, freq)
    return x0 * bidirectional_fft_conv(x1 * v, h_fwd, h_bwd, bias)


def neighbourhood_attention(q, k, v, k_ctx, v_ctx, rpb):
    bsz, seq, _ = q.shape
    rows = seq // GRID_W
    wr = min(NA_WIN_ROWS, rows)
    n_cb = GRID_W // NA_QBLOCK_COLS
    scale = NA_HEAD_DIM ** -0.5
    grid = lambda t: t.reshape(bsz, rows, GRID_W, NA_HEADS, NA_HEAD_DIM)
    qg, kg, vg = grid(q), grid(k), grid(v)
    kc = k_ctx.reshape(bsz, -1, NA_HEADS, NA_HEAD_DIM)
    vc = v_ctx.reshape(bsz, -1, NA_HEADS, NA_HEAD_DIM)
    qcol = np.arange(GRID_W).reshape(n_cb, NA_QBLOCK_COLS)
    win_start = np.clip(qcol - NA_WIN_COLS // 2, 0, GRID_W - NA_WIN_COLS)
    kb_start = np.clip(np.arange(n_cb) * NA_QBLOCK_COLS - NA_WIN_COLS // 2, 0, GRID_W - NA_KBLOCK_COLS)
    kcol = kb_start[:, None] + np.arange(NA_KBLOCK_COLS)[None, :]
    col_ok = (kcol[:, None, :] >= win_start[:, :, None]) & (kcol[:, None, :] < win_start[:, :, None] + NA_WIN_COLS)
    dc_idx = np.clip(kcol[:, None, :] - qcol[:, :, None] + NA_WIN_COLS - 1, 0, 2 * NA_WIN_COLS - 2)
    col_ok = jnp.asarray(col_ok)
    dc_idx = jnp.asarray(dc_idx)
    n_loc = wr * NA_KBLOCK_COLS

    def one_row(r):
        rs = jnp.clip(r - wr // 2, 0, rows - wr)
        q_r = lax.dynamic_index_in_dim(qg, r, axis=1, keepdims=False)
        q_blk = (q_r * scale).reshape(bsz, n_cb, NA_QBLOCK_COLS, NA_HEADS, NA_HEAD_DIM)
        k_band = lax.dynamic_slice_in_dim(kg, rs, wr, axis=1)
        v_band = lax.dynamic_slice_in_dim(vg, rs, wr, axis=1)
        k_blk = k_band[:, :, kcol]
        v_blk = v_band[:, :, kcol]
        s_loc = jnp.einsum('bjqhd,brjkhd->bhjqrk', q_blk, k_blk).astype(jnp.float32)
        dr_idx = rs + jnp.arange(wr) - r + NA_WIN_ROWS - 1
        bias = rpb[:, dr_idx][:, :, dc_idx].astype(jnp.float32)
        s_loc = s_loc + bias.transpose(0, 2, 3, 1, 4)
        s_loc = jnp.where(col_ok[:, :, None, :], s_loc, NEG_BIG)
        s_loc = s_loc.reshape(bsz, NA_HEADS, n_cb, NA_QBLOCK_COLS, n_loc)
        s_ctx = jnp.einsum('bjqhd,bchd->bhjqc', q_blk, kc).astype(jnp.float32)
        p = jax.nn.softmax(jnp.concatenate([s_loc, s_ctx], axis=-1), axis=-1).astype(v.dtype)
        p_loc = p[..., :n_loc].reshape(bsz, NA_HEADS, n_cb, NA_QBLOCK_COLS, wr, NA_KBLOCK_COLS)
        p_ctx = p[..., n_loc:]
        o = (jnp.einsum('bhjqrk,brjkhd->bjqhd', p_loc, v_blk)
             + jnp.einsum('bhjqc,bchd->bjqhd', p_ctx, vc))
        return o.reshape(bsz, GRID_W, NA_WIDTH)

    out = lax.map(one_row, jnp.arange(rows))
    return out.transpose(1, 0, 2, 3).reshape(bsz, seq, NA_WIDTH)


def conformer_conv(t, w_pw1, b_pw1, w_dw, b_dw, ln_g, ln_b, w_pw2, b_pw2):
    a, g = jnp.split(t @ w_pw1 + b_pw1, 2, axis=-1)
    u = a * jax.nn.sigmoid(g)
    u = dwconv(u, w_dw, b_dw)
    u = jax.nn.silu(layer_norm(u, ln_g, ln_b))
    return u @ w_pw2 + b_pw2


def conv_ffn(t, w_up, w_dw, b_dw, w_down):
    z = dwconv(t @ w_up, w_dw, b_dw)
    g, u = jnp.split(z, 2, axis=-1)
    return (jax.nn.silu(g) * u) @ w_down


def setup_inputs(seed: int = 0) -> dict:
    key = jax.random.key(seed)
    ks = iter(jax.random.split(key, 48))
    D = D_MODEL
    nz = 3 * HY_WIDTH + 3 * NA_WIDTH

    def nrm(shape, scale):
        return scale * jax.random.normal(next(ks), shape, jnp.float32)

    def gain(shape):
        return 1.0 + nrm(shape, 0.1)

    return {
        'x': nrm((BATCH, SEQ, D), 1.0),
        'c': nrm((BATCH, D), 1.0),
        'ctx': nrm((BATCH, CTX_LEN, D), 1.0),
        'c_ctx': nrm((D,), 1.0),
        'w_mod': nrm((DEPTH, D, 6 * D), 0.5 * D ** -0.5),
        'b_mod': nrm((DEPTH, 6 * D), 0.02),
        'g_mix_pre': gain((DEPTH, D)),
        'g_mix_post': gain((DEPTH, D)),
        'g_ffn_pre': gain((DEPTH, D)),
        'g_ffn_post': gain((DEPTH, D)),
        'w_in': nrm((N_EVEN, D, nz), D ** -0.5),
        'w_out': nrm((N_EVEN, HY_WIDTH + NA_WIDTH, D), (HY_WIDTH + NA_WIDTH) ** -0.5),
        'hy_short_w': nrm((N_EVEN, HY_SHORT_K, 3 * HY_WIDTH), HY_SHORT_K ** -0.5),
        'hy_short_b': nrm((N_EVEN, 3 * HY_WIDTH), 0.02),
        'hy_f_w1': nrm((N_EVEN, HY_EMB_DIM, HY_FILTER_HIDDEN), HY_EMB_DIM ** -0.5),
        'hy_f_b1': nrm((N_EVEN, HY_FILTER_HIDDEN), 0.02),
        'hy_f_w2': nrm((N_EVEN, HY_FILTER_HIDDEN, HY_FILTER_HIDDEN), HY_FILTER_HIDDEN ** -0.5),
        'hy_f_b2': nrm((N_EVEN, HY_FILTER_HIDDEN), 0.02),
        'hy_f_w3': nrm((N_EVEN, HY_FILTER_HIDDEN, HY_FILTER_HIDDEN), HY_FILTER_HIDDEN ** -0.5),
        'hy_f_b3': nrm((N_EVEN, HY_FILTER_HIDDEN), 0.02),
        'hy_f_w4': nrm((N_EVEN, HY_FILTER_HIDDEN, 2 * HY_WIDTH), HY_FILTER_HIDDEN ** -0.5),
        'hy_f_freq': gain((N_EVEN, 3, HY_FILTER_HIDDEN)),
        'hy_bias': nrm((N_EVEN, HY_WIDTH), 0.5),
        'na_rpb': nrm((N_EVEN, NA_HEADS, 2 * NA_WIN_ROWS - 1, 2 * NA_WIN_COLS - 1), 0.1),
        'cf_w_pw1': nrm((N_ODD, D, 2 * CONF_WIDTH), D ** -0.5),
        'cf_b_pw1': nrm((N_ODD, 2 * CONF_WIDTH), 0.02),
        'cf_w_dw': nrm((N_ODD, CONF_K, CONF_WIDTH), CONF_K ** -0.5),
        'cf_b_dw': nrm((N_ODD, CONF_WIDTH), 0.02),
        'cf_ln_g': gain((N_ODD, CONF_WIDTH)),
        'cf_ln_b': nrm((N_ODD, CONF_WIDTH), 0.02),
        'cf_w_pw2': nrm((N_ODD, CONF_WIDTH, D), CONF_WIDTH ** -0.5),
        'cf_b_pw2': nrm((N_ODD, D), 0.02),
        'ffn_w_up': nrm((DEPTH, D, 2 * D_FF), D ** -0.5),
        'ffn_w_dw': nrm((DEPTH, FFN_CONV_K, 2 * D_FF), FFN_CONV_K ** -0.5),
        'ffn_b_dw': nrm((DEPTH, 2 * D_FF), 0.02),
        'ffn_w_down': nrm((DEPTH, D_FF, D), D_FF ** -0.5),
    }


def reference(x, c, ctx, c_ctx, w_mod, b_mod, g_mix_pre, g_mix_post, g_ffn_pre, g_ffn_post,
              w_in, w_out, hy_short_w, hy_short_b, hy_f_w1, hy_f_b1, hy_f_w2, hy_f_b2,
              hy_f_w3, hy_f_b3, hy_f_w4, hy_f_freq, hy_bias, na_rpb,
              cf_w_pw1, cf_b_pw1, cf_w_dw, cf_b_dw, cf_ln_g, cf_ln_b, cf_w_pw2, cf_b_pw2,
              ffn_w_up, ffn_w_dw, ffn_b_dw, ffn_w_down):
    h = x
    ctx_stream = ctx
    for layer in range(DEPTH):
        shift1, scale1, gate1, shift2, scale2, gate2 = adaln_params(c, w_mod[layer], b_mod[layer])
        hn = rms_norm(h, g_mix_pre[layer]) * (1 + scale1) + shift1
        if layer % 2 == 0:
            e = layer // 2
            m_ctx = jax.nn.silu(c_ctx) @ w_mod[layer][:, :2 * D_MODEL] + b_mod[layer][:2 * D_MODEL]
            cshift, cscale = jnp.split(m_ctx, 2)
            cn = rms_norm(ctx_stream, g_mix_pre[layer]) * (1 + cscale) + cshift
            k_ctx, v_ctx = jnp.split(cn @ w_in[e][:, 3 * HY_WIDTH + NA_WIDTH:], 2, axis=-1)
            z = hn @ w_in[e]
            y_hy = hyena_mixer(z[..., :3 * HY_WIDTH], hy_short_w[e], hy_short_b[e],
                               hy_f_w1[e], hy_f_b1[e], hy_f_w2[e], hy_f_b2[e],
                               hy_f_w3[e], hy_f_b3[e], hy_f_w4[e], hy_f_freq[e], hy_bias[e])
            q, k, v = jnp.split(z[..., 3 * HY_WIDTH:], 3, axis=-1)
            y_na = neighbourhood_attention(q, k, v, k_ctx, v_ctx, na_rpb[e])
            y = jnp.concatenate([y_hy, y_na], axis=-1) @ w_out[e]
        else:
            o = layer // 2
            y = conformer_conv(hn, cf_w_pw1[o], cf_b_pw1[o], cf_w_dw[o], cf_b_dw[o],
                               cf_ln_g[o], cf_ln_b[o], cf_w_pw2[o], cf_b_pw2[o])
        h = h + gate1 * rms_norm(y, g_mix_post[layer])
        hn = rms_norm(h, g_ffn_pre[layer]) * (1 + scale2) + shift2
        y = conv_ffn(hn, ffn_w_up[layer], ffn_w_dw[layer], ffn_b_dw[layer], ffn_w_down[layer])
        h = h + gate2 * rms_norm(y, g_ffn_post[layer])
    return h
```

```python
import contextlib
import math
import numpy as np
import ml_dtypes
import concourse.bass as bass
import concourse.mybir as mybir
from concourse.bass_utils import run_bass_kernel_spmd

F32 = mybir.dt.float32
BF16 = mybir.dt.bfloat16
AF = mybir.ActivationFunctionType
ALU = mybir.AluOpType

T = 4096
D = 1024
KC = 8
NCORES = 8
EPS = 1e-6
DFF = 2816
NFF = 22
TT_CONV = [456] * 8 + [448]
TWO_PI = 2.0 * math.pi


class Res:
    __slots__ = ("name", "w", "r")

    def __init__(self, name):
        self.name = name
        self.w = {}
        self.r = {}


def _merge(dst, src):
    for k, v in src.items():
        if dst.get(k, 0) < v:
            dst[k] = v


class Prog:
    NDS = 12

    def __init__(self, nc):
        self.nc = nc
        self.eng = {"pe": nc.tensor, "act": nc.scalar, "dve": nc.vector, "pool": nc.gpsimd, "sp": nc.sync}
        self.csem = {e: nc.alloc_semaphore("c_" + e) for e in ("pe", "act", "dve", "pool")}
        self.ccnt = {e: 0 for e in self.csem}
        self.dsem = {q: [nc.alloc_semaphore("d_%s%d" % (q, i)) for i in range(self.NDS)] for q in ("sp", "gq")}
        self.dcnt = {q: [0] * self.NDS for q in self.dsem}
        self.dnext = {q: 0 for q in self.dsem}
        self.seen = {s: {} for s in self.eng}
        self.pending = {e: False for e in self.csem}
        self.nres = 0

    def res(self, name=None):
        self.nres += 1
        return Res(name or ("r%d" % self.nres))

    def _handle(self, key):
        if key[0] == "c":
            return self.csem[key[1]]
        return self.dsem[key[1]][key[2]]

    def _wait(self, stream, deps):
        seen = self.seen[stream]
        for key, val in deps.items():
            if val <= 0 or seen.get(key, 0) >= val:
                continue
            if stream == "pe" and key == ("c", "pe"):
                continue
            self.eng[stream].wait_ge(self._handle(key), val)
            seen[key] = val

    def _deps(self, reads, writes):
        deps = {}
        for r in reads:
            _merge(deps, r.w)
        for w in writes:
            _merge(deps, w.w)
            _merge(deps, w.r)
        return deps

    def op(self, e, fn, reads=(), writes=(), inc=True):
        self._wait(e, self._deps(reads, writes))
        ins = fn(self.eng[e])
        key = ("c", e)
        val = self.ccnt[e] + 1
        if inc:
            ins.then_inc(self.csem[e], 1)
            self.ccnt[e] = val
            self.pending[e] = False
        else:
            self.pending[e] = True
        for r in reads:
            if r.r.get(key, 0) < val:
                r.r[key] = val
        for w in writes:
            if w.w.get(key, 0) < val:
                w.w[key] = val
        return ins

    def dma(self, q, out, in_, reads=(), writes=(), **kw):
        stream = "sp" if q == "sp" else "pool"
        i = self.dnext[q]
        self.dnext[q] = (i + 1) % self.NDS
        key = ("d", q, i)
        deps = self._deps(reads, writes)
        prev = self.dcnt[q][i] * 16
        if prev and deps.get(key, 0) < prev:
            deps[key] = prev
        self._wait(stream, deps)
        ins = self.eng[stream].dma_start(out=out, in_=in_, **kw)
        self.dcnt[q][i] += 1
        val = self.dcnt[q][i] * 16
        ins.then_inc(self.dsem[q][i], 16)
        for r in reads:
            if r.r.get(key, 0) < val:
                r.r[key] = val
        for w in writes:
            if w.w.get(key, 0) < val:
                w.w[key] = val
        return ins

    def barrier(self):
        for e in self.pending:
            assert not self.pending[e], "pending un-inc'd op on " + e
        allv = {("c", e): v for e, v in self.ccnt.items()}
        for q in self.dsem:
            for i in range(self.NDS):
                allv[("d", q, i)] = self.dcnt[q][i] * 16
        for s in self.eng:
            self._wait(s, allv)


def _bf(a):
    return np.ascontiguousarray(a.astype(np.float32)).astype(ml_dtypes.bfloat16)


def _zpad(G):
    out = np.zeros((128,) + G.shape[1:], G.dtype)
    out[0:64, 0::2] = G[:, 0::2]
    out[64:128, 1::2] = G[:, 1::2]
    return out


def make_consts():
    N = 8192
    t1 = np.arange(64)[:, None, None].astype(np.float64)
    t2 = np.arange(64)[None, :, None].astype(np.float64)
    f1 = np.arange(128)[None, None, :].astype(np.float64)
    phi = 2 * np.pi * (f1 + 0.5) * (64 * t1 + t2) / N
    Gr = np.cos(phi)
    Gi = -np.sin(phi)
    tt = np.arange(64)[:, None].astype(np.float64)
    f2 = np.arange(32)[None, :].astype(np.float64)
    Cm = np.cos(2 * np.pi * tt * f2 / 64)
    Sm = np.sin(2 * np.pi * tt * f2 / 64)

    def blk(rr, ri, ir, ii):
        return np.block([[rr, ri], [ir, ii]])

    F2 = np.concatenate([blk(Cm, -Sm, Sm, Cm), blk(Sm, Cm, -Cm, Sm)], 1)
    Kf = np.concatenate([blk(Cm, Cm, Sm, Sm), blk(-Sm, -Sm, Cm, Cm)], 1)
    Kb = np.concatenate([blk(Cm, Cm, Sm, Sm), blk(Sm, Sm, -Cm, -Cm)], 1)
    CT = Cm.T
    ST = Sm.T
    Ah = np.block([[CT, ST], [-ST, CT]])
    A = np.concatenate([Ah, Ah], 0)
    RBr = (2.0 / N) * np.transpose(Gr, (2, 1, 0))
    RBi = (2.0 / N) * np.transpose(Gi, (2, 1, 0))
    c = {
        "c_gr": _bf(_zpad(Gr).reshape(128, 64 * 128)),
        "c_gi": _bf(_zpad(Gi).reshape(128, 64 * 128)),
        "c_f2": _bf(F2), "c_kf": _bf(Kf), "c_kb": _bf(Kb), "c_a": _bf(A),
        "c_rbr": _bf(RBr.reshape(128, 64 * 64)),
        "c_rbi": _bf(RBi.reshape(128, 64 * 64)),
        "c_ident": _bf(np.eye(128)),
    }
    L = T
    t01 = np.linspace(0.0, 1.0, L, dtype=np.float32)[:, None]
    w_pos = (np.float32(2.0 * math.pi) * np.arange(L, dtype=np.float32) / np.float32(L)).astype(np.float32)
    f = np.linspace(1e-4, 15, 16, dtype=np.float32)
    ang = (w_pos[:, None] * f[None, :]).astype(np.float32)
    z = np.concatenate([t01, np.cos(ang), -np.sin(ang)], axis=-1).astype(np.float32)
    c["c_zT"] = np.ascontiguousarray(z.T)
    deltas = np.abs(np.linspace(math.log(1e-2) / 0.3, math.log(1e-2) / 1.5, 512, dtype=np.float32))
    decay = np.exp(-t01 * deltas[None, :]).astype(np.float32)
    c["c_decay"] = np.ascontiguousarray(np.transpose(decay.reshape(64, 64, 512), (1, 0, 2)).reshape(32, 128, 512))
    return c


def make_rpb_table(rpb):
    q = np.arange(64)
    ws = np.clip(q - 8, 0, 48)
    kc = np.arange(64)
    ok = (kc[None, :] >= ws[:, None]) & (kc[None, :] < ws[:, None] + 16)
    dc = np.clip(kc[None, :] - q[:, None] + 15, 0, 30)
    g = rpb[:, :, dc]
    g = np.where(ok[None, None], g, np.float32(-1e30)).astype(np.float32)
    out = np.empty((8, 2, 64, 14, 64), np.float32)
    for a in range(2):
        out[:, a] = np.transpose(g[:, a:a + 14], (0, 3, 1, 2))
    return np.ascontiguousarray(out.reshape(8, 128, 14 * 64))


def col(v, nchunk):
    v = np.asarray(v, np.float32)
    if v.ndim == 1:
        return np.ascontiguousarray(v.reshape(nchunk, 128).T)
    r = v.shape[0]
    return np.ascontiguousarray(np.transpose(v.reshape(r, nchunk, 128), (2, 1, 0)))


INPUT_SPECS = {}


def build(debug=False, stop=None, skip_steps=None):
    nc = bass.Bass("TRN2", target_bir_lowering=False)
    pg = Prog(nc)
    dt_np = {F32: np.float32, BF16: ml_dtypes.bfloat16}
    ins = {}

    def inp(name, shape, dt=F32):
        INPUT_SPECS[name] = (tuple(shape), dt_np[dt])
        ins[name] = nc.dram_tensor(name, list(shape), dt, kind="ExternalInput").ap()
        return ins[name]

    dbg_names = []

    def scratch(name, shape, dt):
        kind = "ExternalOutput" if debug else "Internal"
        if debug:
            dbg_names.append(name)
        return nc.dram_tensor(name, list(shape), dt, kind=kind).ap()

    x = inp("x", [T, D])
    ccol = inp("ccol", [128, 2, 8])
    ctx = inp("ctx", [256, D])
    w_mod = inp("w_mod", [2, D, 6 * D])
    b_mod = inp("b_mod", [2, 6 * D])
    gvec = inp("gvec", [2, 4, D])
    w_in = inp("w_in", [D, 3072])
    w_out = inp("w_out", [D, D])
    hsw_col = inp("hsw_col", [128, 12, 3])
    hsb_col = inp("hsb_col", [128, 12])
    hf_w1 = inp("hf_w1", [33, 64])
    hf_w2 = inp("hf_w2", [64, 64])
    hf_w3 = inp("hf_w3", [64, 64])
    hf_w4 = inp("hf_w4", [64, 1024])
    hf_b_col = inp("hf_b_col", [64, 3])
    hf_freq_col = inp("hf_freq_col", [64, 3])
    hyb_col = inp("hyb_col", [128, 4])
    rpb_t2 = inp("rpb_t2", [8, 128, 14 * 64])
    cf_w_pw1 = inp("cf_w_pw1", [D, 2048])
    cf_b1_col = inp("cf_b1_col", [128, 16])
    cf_wdw_col = inp("cf_wdw_col", [128, 8, 31])
    cf_bdw_col = inp("cf_bdw_col", [128, 8])
    cf_lng_col = inp("cf_lng_col", [128, 8])
    cf_lnb_col = inp("cf_lnb_col", [128, 8])
    cf_w_pw2 = inp("cf_w_pw2", [D, D])
    cf_b_pw2 = inp("cf_b_pw2", [1, D])
    ffn_w_up = inp("ffn_w_up", [2, D, 2 * DFF])
    ffn_wdw_col = inp("ffn_wdw_col", [2, 128, 44, 3])
    ffn_bdw_col = inp("ffn_bdw_col", [2, 128, 44])
    ffn_w_down = inp("ffn_w_down", [2, DFF, D])
    c_gr = inp("c_gr", [128, 64 * 128], BF16)
    c_gi = inp("c_gi", [128, 64 * 128], BF16)
    c_f2 = inp("c_f2", [128, 128], BF16)
    c_kf = inp("c_kf", [128, 128], BF16)
    c_kb = inp("c_kb", [128, 128], BF16)
    c_a = inp("c_a", [128, 128], BF16)
    c_rbr = inp("c_rbr", [128, 64 * 64], BF16)
    c_rbi = inp("c_rbi", [128, 64 * 64], BF16)
    c_ident = inp("c_ident", [128, 128], BF16)
    c_zT = inp("c_zT", [33, T])
    c_decay = inp("c_decay", [32, 128, 512])

    out = nc.dram_tensor("out", [T, D], F32, kind="ExternalOutput").ap()

    hbuf = scratch("hbuf", [T, D], F32)
    modrow = scratch("modrow", [2, 8, D], F32)
    x0T = scratch("x0T", [512, T], BF16)
    vpT = scratch("vpT", [512, T], BF16)
    qT = scratch("qT", [512, T], BF16)
    kT = scratch("kT", [512, T], BF16)
    v_d = scratch("v_d", [T, 512], BF16)
    kcT = scratch("kcT", [512, 256], BF16)
    vc_d = scratch("vc_d", [256, 512], BF16)
    BdK = scratch("BdK", [128, 128, 1024], BF16)
    KKd = scratch("KKd", [128, 128, 512], BF16)
    Bd = scratch("Bd", [128, 128, 512], BF16)
    Zd = scratch("Zd", [128, 128, 512], BF16)
    yT = scratch("yT", [D, T], BF16)
    actT = scratch("actT", [DFF, T], BF16)
    u2T = scratch("u2T", [D, T], F32)

    Rd = {n: pg.res("d_" + n) for n in
          ["hbuf", "modrow", "x0T", "vpT", "qT", "kT", "v_d", "kcT", "vc_d", "BdK", "KKd", "Bd", "Zd",
           "yT", "actT", "u2T", "out"]}

    es = contextlib.ExitStack()
    with es:
        uniq = [0]

        def sb(name, shape, dt, stack=es):
            uniq[0] += 1
            return stack.enter_context(nc.sbuf_tensor("%s_%d" % (name, uniq[0]), list(shape), dt))

        psall = es.enter_context(nc.psum_tensor("psall", [128, 8 * 512], F32))
        ps = [psall[:, i * 512:(i + 1) * 512] for i in range(8)]
        Rps = [pg.res("ps%d" % i) for i in range(8)]
        H = {}
        R_hnT = pg.res("hnT")
        ident = sb("ident", [128, 128], BF16)
        R_const = pg.res("const")
        epsc = sb("epsc", [128, 1], F32)
        pg.dma("sp", ident[:], c_ident, writes=[R_const])
        pg.op("dve", lambda e: e.memset(epsc[:], EPS), writes=[R_const])

        def hn_open(stack):
            hnT = sb("hnT", [128, KC, T + 2], BF16, stack)
            H["hnT"] = hnT
            pg.op("dve", lambda e: e.memset(hnT[:, :, 0:1], 0.0), writes=[R_hnT])
            pg.op("dve", lambda e: e.memset(hnT[:, :, T + 1:T + 2], 0.0), writes=[R_hnT])

        state = {"psi": 0}

        def next_ps():
            i = state["psi"]
            state["psi"] = (i + 1) % 8
            return ps[i], Rps[i]

        def skew(n_iter, stage_fns, reverse=True):
            S = len(stage_fns)
            for it in range(n_iter + S - 1):
                for si in (range(S - 1, -1, -1) if reverse else range(S)):
                    i = it - si
                    if 0 <= i < n_iter:
                        stage_fns[si](i)

        def rstd_from_ss(ss_ap, lnv_ap, rstd_ap, R):
            pg.op("act", lambda e: e.activation(out=lnv_ap, in_=ss_ap, func=AF.Ln, bias=epsc[0:ss_ap.shape[0], :],
                                                scale=1.0 / D), reads=[R, R_const], writes=[R])
            pg.op("act", lambda e: e.activation(out=rstd_ap, in_=lnv_ap, func=AF.Exp, scale=-0.5),
                  reads=[R], writes=[R])

        def phase_mod(l):
            with contextlib.ExitStack() as st:
                cc = sb("m_cc", [128, 2, 8], F32, st)
                lhs = sb("m_lhs", [128, KC, 33], BF16, st)
                mrow = sb("m_row", [33, 6 * D], F32, st)
                brow = sb("m_brow", [33, 6 * D], F32, st)
                grow = sb("m_grow", [33, 4, D], F32, st)
                orow = sb("m_orow", [33, 6, D], F32, st)
                wt = [sb("m_w%d" % i, [128, KC, 512], BF16, st) for i in range(2)]
                Rw = [pg.res() for _ in range(2)]
                R = pg.res("modtmp")
                pg.dma("sp", cc[:], ccol, writes=[R])
                pg.dma("sp", brow[0:1, :], b_mod[l:l + 1, :], writes=[R])
                pg.dma("sp", brow[32:33, :], b_mod[l:l + 1, :], writes=[R])
                pg.dma("sp", grow[0:1, :, :], gvec[l:l + 1, :, :], writes=[R])
                pg.dma("sp", grow[32:33, :, :], gvec[l:l + 1, :, :], writes=[R])
                pg.op("dve", lambda e: e.memset(lhs[:], 0.0), writes=[R])
                pg.op("act", lambda e: e.activation(out=lhs[:, :, 0], in_=cc[:, 0, :], func=AF.Silu),
                      reads=[R], writes=[R])
                pg.op("act", lambda e: e.activation(out=lhs[:, :, 32], in_=cc[:, 1, :], func=AF.Silu),
                      reads=[R], writes=[R])
                wsrc = w_mod[l].rearrange("(kc p) n -> p kc n", p=128)
                for cg in range(12):
                    b = cg % 2
                    pg.dma("gq", wt[b][:], wsrc[:, :, cg * 512:(cg + 1) * 512], writes=[Rw[b]])
                    pt, Rp = next_ps()
                    for kc in range(KC):
                        pg.op("pe", lambda e: e.matmul(pt[0:33, :], lhs[:, kc, :], wt[b][:, kc, :],
                                                       start=(kc == 0), stop=(kc == KC - 1)),
                              reads=[R, Rw[b]], writes=[Rp], inc=(kc == KC - 1))
                    pg.op("act", lambda e: e.activation(out=mrow[:, cg * 512:(cg + 1) * 512], in_=pt[0:33, :],
                                                        func=AF.Identity), reads=[Rp], writes=[R])
                for p0 in (0, 32):
                    pg.op("dve", lambda e: e.tensor_tensor(out=mrow[p0:p0 + 1, :], in0=mrow[p0:p0 + 1, :],
                                                           in1=brow[p0:p0 + 1, :], op=ALU.add),
                          reads=[R], writes=[R])

                def seg(p0, i):
                    return mrow[p0:p0 + 1, i * D:(i + 1) * D]
                for (dst, sc_i, g_i) in ((0, 1, 0), (3, 4, 2)):
                    pg.op("dve", lambda e: e.scalar_tensor_tensor(out=orow[0:1, dst, :], in0=seg(0, sc_i), scalar=1.0,
                                                                  in1=grow[0:1, g_i, :], op0=ALU.add, op1=ALU.mult),
                          reads=[R], writes=[R])
                for (dst, sh_i) in ((1, 0), (4, 3)):
                    pg.op("dve", lambda e: e.tensor_copy(out=orow[0:1, dst, :], in_=seg(0, sh_i)),
                          reads=[R], writes=[R])
                for (dst, gt_i, g_i) in ((2, 2, 1), (5, 5, 3)):
                    pg.op("dve", lambda e: e.tensor_tensor(out=orow[0:1, dst, :], in0=seg(0, gt_i),
                                                           in1=grow[0:1, g_i, :], op=ALU.mult),
                          reads=[R], writes=[R])
                pg.op("dve", lambda e: e.scalar_tensor_tensor(out=orow[32:33, 0, :], in0=seg(32, 1), scalar=1.0,
                                                              in1=grow[32:33, 0, :], op0=ALU.add, op1=ALU.mult),
                      reads=[R], writes=[R])
                pg.op("dve", lambda e: e.tensor_copy(out=orow[32:33, 1, :], in_=seg(32, 0)), reads=[R], writes=[R])
                pg.dma("sp", modrow[l:l + 1, 0:6, :], orow[0:1, :, :], reads=[R], writes=[Rd["modrow"]])
                pg.dma("sp", modrow[l:l + 1, 6:8, :], orow[32:33, 0:2, :], reads=[R], writes=[Rd["modrow"]])
                pg.barrier()

        def load_bc(dst, l, row, R):
            pg.dma("sp", dst[:], modrow[l, row:row + 1, :].partition_broadcast(128)[:, 0, :],
                   reads=[Rd["modrow"]], writes=[R])

        def phase_prenorm(src, Rsrc, ntok, l, row_gs, row_sh, dstT, R_dst, off, ext_stack=None):
            with contextlib.ExitStack() as st_own:
                st = ext_stack if ext_stack is not None else st_own
                gs = sb("pn_gs", [128, D], F32, st)
                sh = sb("pn_sh", [128, D], BF16, st)
                Rg = pg.res()
                load_bc(gs, l, row_gs, Rg)
                pg.dma("gq", sh[:], modrow[l, row_sh:row_sh + 1, :].partition_broadcast(128)[:, 0, :],
                       reads=[Rd["modrow"]], writes=[Rg])
                NB = 2
                NH = 4
                hin = [sb("pn_h%d" % i, [128, D], F32, st) for i in range(NH)]
                t1 = [sb("pn_t1%d" % i, [128, D], F32, st) for i in range(NB)]
                t2 = [sb("pn_t2%d" % i, [128, D], BF16, st) for i in range(NB)]
                hn = [sb("pn_hn%d" % i, [128, D], BF16, st) for i in range(NB)]
                junk = sb("pn_junk", [128, D], BF16, st)
                small = [sb("pn_s%d" % i, [128, 4], F32, st) for i in range(NB)]
                Rh = [pg.res() for _ in range(NH)]
                Rt1 = [pg.res() for _ in range(NB)]
                Rt2 = [pg.res() for _ in range(NB)]
                Rhn = [pg.res() for _ in range(NB)]
                Rs = [pg.res() for _ in range(NB)]
                Rj = pg.res()
                ntile = ntok // 128
                for i0_ in range(min(2, ntile)):
                    pg.dma("sp", hin[i0_ % NH][:], src[i0_ * 128:(i0_ + 1) * 128, :], reads=[Rsrc],
                           writes=[Rh[i0_ % NH]])

                def s0(i):
                    i2 = i + 2
                    if i2 < ntile:
                        pg.dma("sp", hin[i2 % NH][:], src[i2 * 128:(i2 + 1) * 128, :], reads=[Rsrc],
                               writes=[Rh[i2 % NH]])

                def s1(i):
                    b = i % NB
                    hb_ = i % NH
                    pg.op("act", lambda e: e.activation(out=junk[:], in_=hin[hb_][:], func=AF.Square,
                                                        accum_out=small[b][:, 0:1]),
                          reads=[Rh[hb_]], writes=[Rj, Rs[b]])
                    rstd_from_ss(small[b][:, 0:1], small[b][:, 1:2], small[b][:, 2:3], Rs[b])
                    pg.op("dve", lambda e: e.tensor_scalar(out=t1[b][:], in0=hin[hb_][:], scalar1=small[b][:, 2:3],
                                                           scalar2=None, op0=ALU.mult),
                          reads=[Rh[hb_], Rs[b]], writes=[Rt1[b]])

                def s2(i):
                    b = i % NB
                    pg.op("pool", lambda e: e.tensor_tensor(out=t2[b][:], in0=t1[b][:], in1=gs[:], op=ALU.mult),
                          reads=[Rt1[b], Rg], writes=[Rt2[b]])

                def s3(i):
                    b = i % NB
                    pg.op("dve", lambda e: e.tensor_tensor(out=hn[b][:], in0=t2[b][:], in1=sh[:], op=ALU.add),
                          reads=[Rt2[b], Rg], writes=[Rhn[b]])

                def s4(i):
                    b = i % NB
                    ptb = ps[i % 2].bitcast(BF16)
                    for kc in range(KC):
                        pg.op("pe", lambda e: e.transpose(ptb[:, kc * 128:(kc + 1) * 128],
                                                          hn[b][:, kc * 128:(kc + 1) * 128], ident[:]),
                              reads=[Rhn[b], R_const], writes=[Rps[i % 2]], inc=(kc == KC - 1))

                def s5(i):
                    ptb = ps[i % 2].bitcast(BF16)
                    pg.op("dve", lambda e: e.tensor_copy(
                        out=dstT[:, :, off + i * 128: off + (i + 1) * 128],
                        in_=ptb[:].rearrange("p (k t) -> p k t", k=KC)), reads=[Rps[i % 2]], writes=[R_dst])

                skew(ntok // 128, [s0, s1, s2, s3, s4, s5])
                if ext_stack is None:
                    pg.barrier()

        class RU:
            NB = 3

            def __init__(self, st, l, row_gg, tag):
                NB = self.NB
                self.gg = sb(tag + "_gg", [128, D], F32, st)
                self.Rg = pg.res()
                load_bc(self.gg, l, row_gg, self.Rg)
                self.hin = [sb(tag + "_h%d" % i, [128, D], F32, st) for i in range(NB)]
                self.tmp = [sb(tag + "_t%d" % i, [128, D], F32, st) for i in range(NB)]
                self.hout = [sb(tag + "_o%d" % i, [128, D], F32, st) for i in range(NB)]
                self.small = [sb(tag + "_s%d" % i, [128, 8], F32, st) for i in range(NB)]
                self.junk = sb(tag + "_j", [128, 512], BF16, st)
                self.Rh = [pg.res() for _ in range(NB)]
                self.Rt = [pg.res() for _ in range(NB)]
                self.Ro = [pg.res() for _ in range(NB)]
                self.Rs = [pg.res() for _ in range(NB)]
                self.Rj = pg.res()

            def load(self, i, hsrc, Rsrc):
                b = i % self.NB
                pg.dma("sp", self.hin[b][:], hsrc[i * 128:(i + 1) * 128, :], reads=[Rsrc], writes=[self.Rh[b]])

            def scale(self, i, yh, Ry):
                b = i % self.NB
                sm = self.small[b]
                for hf in range(2):
                    pg.op("act", lambda e: e.activation(out=self.junk[:], in_=yh[hf], func=AF.Square,
                                                        accum_out=sm[:, hf:hf + 1]),
                          reads=[Ry[hf]], writes=[self.Rj, self.Rs[b]])
                pg.op("dve", lambda e: e.tensor_tensor(out=sm[:, 2:3], in0=sm[:, 0:1], in1=sm[:, 1:2], op=ALU.add),
                      reads=[self.Rs[b]], writes=[self.Rs[b]])
                rstd_from_ss(sm[:, 2:3], sm[:, 3:4], sm[:, 4:5], self.Rs[b])
                for hf in range(2):
                    cs = slice(hf * 512, (hf + 1) * 512)
                    pg.op("dve", lambda e: e.scalar_tensor_tensor(out=self.tmp[b][:, cs], in0=yh[hf],
                                                                  scalar=sm[:, 4:5], in1=self.gg[:, cs],
                                                                  op0=ALU.mult, op1=ALU.mult),
                          reads=[Ry[hf], self.Rs[b], self.Rg], writes=[self.Rt[b]])

            def finish(self, i, hdst, Rdst):
                b = i % self.NB
                pg.op("pool", lambda e: e.tensor_tensor(out=self.hout[b][:], in0=self.hin[b][:], in1=self.tmp[b][:],
                                                        op=ALU.add),
                      reads=[self.Rh[b], self.Rt[b]], writes=[self.Ro[b]])
                pg.dma("sp", hdst[i * 128:(i + 1) * 128, :], self.hout[b][:], reads=[self.Ro[b]], writes=[Rdst])

        def conv3_evac(pt, Rp, n, wcol, bcol, Rw, t0, t1, dst, Rt, Rdst):
            pg.op("act", lambda e: e.activation(out=t0[:, 0:n], in_=pt[:, 1:n + 1], func=AF.Identity,
                                                bias=bcol, scale=wcol[:, 1:2]),
                  reads=[Rp, Rw], writes=[Rt])
            pg.op("dve", lambda e: e.scalar_tensor_tensor(out=t1[:, 0:n], in0=pt[:, 0:n], scalar=wcol[:, 0:1],
                                                          in1=t0[:, 0:n], op0=ALU.mult, op1=ALU.add),
                  reads=[Rp, Rw, Rt], writes=[Rt])
            pg.op("dve", lambda e: e.scalar_tensor_tensor(out=dst, in0=pt[:, 2:n + 2], scalar=wcol[:, 2:3],
                                                          in1=t1[:, 0:n], op0=ALU.mult, op1=ALU.add),
                  reads=[Rp, Rw, Rt], writes=[Rdst])

        def load_w(dst, W, c0, n, R):
            pg.dma("gq", dst[:, :, 0:n], W.rearrange("(kc p) n -> p kc n", p=128)[:, :, c0:c0 + n], writes=[R])

        def phase_proj0():
            hnT = H["hnT"]
            with contextlib.ExitStack() as st:
                hsw = sb("p0_hsw", [128, 12, 3], F32, st)
                hsb = sb("p0_hsb", [128, 12], F32, st)
                Rc = pg.res()
                pg.dma("sp", hsw[:], hsw_col, writes=[Rc])
                pg.dma("sp", hsb[:], hsb_col, writes=[Rc])
                x1s = sb("p0_x1s", [128, 4, T], BF16, st)
                R_x1 = pg.res()
                wt = [sb("p0_w%d" % i, [128, KC, 128], BF16, st) for i in range(3)]
                Rw = [pg.res() for _ in range(3)]
                stg = [sb("p0_stg%d" % i, [128, T], BF16, st) for i in range(2)]
                Rstg = [pg.res() for _ in range(2)]
                t0 = [sb("p0_t0%d" % i, [128, 512], F32, st) for i in range(2)]
                t1 = [sb("p0_t1%d" % i, [128, 512], F32, st) for i in range(2)]
                t2 = [sb("p0_t2%d" % i, [128, 512], F32, st) for i in range(2)]
                Rt = [pg.res() for _ in range(2)]
                Rt2 = [pg.res() for _ in range(2)]
                order = [4, 5, 6, 7, 8, 9, 10, 11, 0, 1, 2, 3]
                nblk = 0
                load_w(wt[0], w_in, order[0] * 128, 128, Rw[0])
                for ci, c in enumerate(order):
                    b = ci % 2
                    wb = ci % 3
                    if ci + 1 < len(order):
                        load_w(wt[(ci + 1) % 3], w_in, order[ci + 1] * 128, 128, Rw[(ci + 1) % 3])
                    s = 0
                    for n in TT_CONV:
                        tb = nblk % 2
                        nblk += 1
                        pt, Rp = next_ps()
                        for kc in range(KC):
                            pg.op("pe", lambda e: e.matmul(pt[:, 0:n + 2], wt[wb][:, kc, :], hnT[:, kc, s:s + n + 2],
                                                           start=(kc == 0), stop=(kc == KC - 1)),
                                  reads=[Rw[wb], R_hnT], writes=[Rp], inc=(kc == KC - 1))
                        if 4 <= c < 8:
                            conv3_evac(pt, Rp, n, hsw[:, c, :], hsb[:, c:c + 1], Rc, t0[tb], t1[tb],
                                       x1s[:, c - 4, s:s + n], Rt[tb], R_x1)
                        elif c >= 8:
                            conv3_evac(pt, Rp, n, hsw[:, c, :], hsb[:, c:c + 1], Rc, t0[tb], t1[tb],
                                       t2[tb][:, 0:n], Rt[tb], Rt2[tb])
                            pg.op("dve", lambda e: e.tensor_tensor(out=stg[b][:, s:s + n], in0=t2[tb][:, 0:n],
                                                                   in1=x1s[:, c - 8, s:s + n], op=ALU.mult),
                                  reads=[Rt2[tb], R_x1], writes=[Rstg[b]])
                        else:
                            conv3_evac(pt, Rp, n, hsw[:, c, :], hsb[:, c:c + 1], Rc, t0[tb], t1[tb],
                                       stg[b][:, s:s + n], Rt[tb], Rstg[b])
                        s += n
                    if c >= 8:
                        pg.dma("sp", vpT[(c - 8) * 128:(c - 7) * 128, :], stg[b][:], reads=[Rstg[b]],
                               writes=[Rd["vpT"]])
                    elif c < 4:
                        pg.dma("sp", x0T[c * 128:(c + 1) * 128, :], stg[b][:], reads=[Rstg[b]], writes=[Rd["x0T"]])
                pg.barrier()
            with contextlib.ExitStack() as st:
                wt = [sb("p1_w%d" % i, [128, KC, 128], BF16, st) for i in range(2)]
                Rw = [pg.res() for _ in range(2)]
                stg = [sb("p1_stg%d" % i, [128, T], BF16, st) for i in range(2)]
                Rstg = [pg.res() for _ in range(2)]
                wk = sb("p1_wk", [128, KC, 512], BF16, st)
                wv = sb("p1_wv", [128, KC, 512], BF16, st)
                Rwk = pg.res()
                load_w(wk, w_in, 2048, 512, Rwk)
                load_w(wv, w_in, 2560, 512, Rwk)
                cnT = sb("p1_cnT", [128, KC, 256], BF16, st)
                R_cn = pg.res()
                phase_prenorm(ctx, pg.res(), 256, 0, 6, 7, cnT, R_cn, 0, ext_stack=st)
                for ci in range(8):
                    b = ci % 2
                    isq = ci < 4
                    c0 = 1536 + ci * 128
                    load_w(wt[b], w_in, c0, 128, Rw[b])
                    for ti in range(8):
                        pt, Rp = next_ps()
                        for kc in range(KC):
                            pg.op("pe", lambda e: e.matmul(pt[:], wt[b][:, kc, :],
                                                           hnT[:, kc, 1 + ti * 512:1 + (ti + 1) * 512],
                                                           start=(kc == 0), stop=(kc == KC - 1)),
                                  reads=[Rw[b], R_hnT], writes=[Rp], inc=(kc == KC - 1))
                        if ti % 2 == 0:
                            pg.op("act", lambda e: e.activation(out=stg[b][:, ti * 512:(ti + 1) * 512], in_=pt[:],
                                                                func=AF.Copy, scale=(0.125 if isq else 1.0)),
                                  reads=[Rp], writes=[Rstg[b]])
                        else:
                            pg.op("dve", lambda e: e.tensor_scalar(out=stg[b][:, ti * 512:(ti + 1) * 512], in0=pt[:],
                                                                   scalar1=(0.125 if isq else 1.0), scalar2=None,
                                                                   op0=ALU.mult),
                                  reads=[Rp], writes=[Rstg[b]])
                    dstT, nm = (qT, "qT") if isq else (kT, "kT")
                    cc = ci % 4
                    pg.dma("sp", dstT[cc * 128:(cc + 1) * 128, :], stg[b][:], reads=[Rstg[b]], writes=[Rd[nm]])
                vst = [sb("p1_vst%d" % i, [128, 512], BF16, st) for i in range(2)]
                Rvst = [pg.res() for _ in range(2)]
                for i in range(32):
                    b = i % 2
                    pt, Rp = next_ps()
                    for kc in range(KC):
                        pg.op("pe", lambda e: e.matmul(pt[:], hnT[:, kc, 1 + i * 128:1 + (i + 1) * 128], wv[:, kc, :],
                                                       start=(kc == 0), stop=(kc == KC - 1)),
                              reads=[Rwk, R_hnT], writes=[Rp], inc=(kc == KC - 1))
                    if i % 2 == 0:
                        pg.op("act", lambda e: e.activation(out=vst[b][:], in_=pt[:], func=AF.Copy),
                              reads=[Rp], writes=[Rvst[b]])
                    else:
                        pg.op("dve", lambda e: e.tensor_copy(out=vst[b][:], in_=pt[:]), reads=[Rp], writes=[Rvst[b]])
                    pg.dma("sp", v_d[i * 128:(i + 1) * 128, :], vst[b][:], reads=[Rvst[b]], writes=[Rd["v_d"]])
                kst = sb("p1_kst", [128, 4, 256], BF16, st)
                Rk = pg.res()
                for cc in range(4):
                    pt, Rp = next_ps()
                    for kc in range(KC):
                        pg.op("pe", lambda e: e.matmul(pt[:, 0:256], wk[:, kc, cc * 128:(cc + 1) * 128], cnT[:, kc, :],
                                                       start=(kc == 0), stop=(kc == KC - 1)),
                              reads=[Rwk, R_cn], writes=[Rp], inc=(kc == KC - 1))
                    pg.op("act", lambda e: e.activation(out=kst[:, cc, :], in_=pt[:, 0:256], func=AF.Copy),
                          reads=[Rp], writes=[Rk])
                pg.dma("sp", kcT.rearrange("(c p) t -> p c t", p=128), kst[:], reads=[Rk], writes=[Rd["kcT"]])
                for i in range(2):
                    pt, Rp = next_ps()
                    for kc in range(KC):
                        pg.op("pe", lambda e: e.matmul(pt[:], cnT[:, kc, i * 128:(i + 1) * 128], wv[:, kc, :],
                                                       start=(kc == 0), stop=(kc == KC - 1)),
                              reads=[Rwk, R_cn], writes=[Rp], inc=(kc == KC - 1))
                    pg.op("act", lambda e: e.activation(out=vst[i][:], in_=pt[:], func=AF.Copy),
                          reads=[Rp], writes=[Rvst[i]])
                    pg.dma("sp", vc_d[i * 128:(i + 1) * 128, :], vst[i][:], reads=[Rvst[i]], writes=[Rd["vc_d"]])
                pg.barrier()

        def phase_filters():
          with contextlib.ExitStack() as st0:
            h3 = sb("f_h3", [64, T], BF16, st0)
            R3 = pg.res()
            with contextlib.ExitStack() as st:
                zT = sb("f_zT", [33, T], F32, st)
                w1 = sb("f_w1", [33, 64], F32, st)
                w2 = sb("f_w2", [64, 64], F32, st)
                w3 = sb("f_w3", [64, 64], F32, st)
                bc = sb("f_b", [64, 3], F32, st)
                fq = sb("f_fq", [64, 3], F32, st)
                fs = sb("f_fs", [64, 3], F32, st)
                fb = sb("f_fb", [64, 3], F32, st)
                hA = sb("f_hA", [64, T], F32, st)
                hB = sb("f_hB", [64, T], F32, st)
                sa = [sb("f_sa%d" % i, [64, 512], F32, st) for i in range(2)]
                sb_ = [sb("f_sb%d" % i, [64, 512], F32, st) for i in range(2)]
                Rs = [pg.res() for _ in range(2)]
                R = pg.res()
                RA, RB = pg.res(), pg.res()
                pg.dma("sp", zT[:], c_zT, writes=[R])
                pg.dma("sp", w1[:], hf_w1, writes=[R])
                pg.dma("sp", w2[:], hf_w2, writes=[R])
                pg.dma("sp", w3[:], hf_w3, writes=[R])
                pg.dma("sp", bc[:], hf_b_col, writes=[R])
                pg.dma("sp", fq[:], hf_freq_col, writes=[R])
                pg.op("dve", lambda e: e.tensor_scalar(out=fs[:], in0=fq[:], scalar1=1.0 / TWO_PI, scalar2=None,
                                                       op0=ALU.mult), reads=[R], writes=[R])
                pg.op("dve", lambda e: e.tensor_tensor(out=fb[:], in0=fs[:], in1=bc[:], op=ALU.mult),
                      reads=[R], writes=[R])
                layers = [(w1, 33, zT, R, hA, RA), (w2, 64, hA, RA, hB, RB), (w3, 64, hB, RB, h3, R3)]
                nb = 0
                for li, (w, K, src, Rsrc, dst, Rdst) in enumerate(layers):
                    for ti in range(8):
                        b = nb % 2
                        nb += 1
                        cs = slice(ti * 512, (ti + 1) * 512)
                        pt, Rp = next_ps()
                        pg.op("pe", lambda e: e.matmul(pt[0:64, :], w[0:K, :], src[0:K, cs], start=True, stop=True),
                              reads=[R, Rsrc], writes=[Rp])
                        pg.op("dve", lambda e: e.tensor_scalar(out=sa[b][:], in0=pt[0:64, :], scalar1=fs[:, li:li + 1],
                                                               scalar2=fb[:, li:li + 1], op0=ALU.mult, op1=ALU.add),
                              reads=[Rp, R], writes=[Rs[b]])
                        pg.op("dve", lambda e: e.scalar_tensor_tensor(out=sb_[b][:], in0=sa[b][:], scalar=0.5,
                                                                      in1=sa[b][:], op0=ALU.is_gt, op1=ALU.subtract),
                              reads=[Rs[b]], writes=[Rs[b]])
                        pg.op("dve", lambda e: e.scalar_tensor_tensor(out=sb_[b][:], in0=sa[b][:], scalar=-0.5,
                                                                      in1=sb_[b][:], op0=ALU.is_lt, op1=ALU.subtract),
                              reads=[Rs[b]], writes=[Rs[b]])
                        pg.op("act", lambda e: e.activation(out=dst[:, cs], in_=sb_[b][:], func=AF.Sin,
                                                            scale=TWO_PI * (1.0 - 1e-6)),
                              reads=[Rs[b]], writes=[Rdst])
                pg.barrier()
            with contextlib.ExitStack() as st:
                R = pg.res()
                w4 = sb("f_w4", [128, 1024], BF16, st)
                h3p = sb("f_h3p", [128, T], BF16, st)
                pg.op("pool", lambda e: e.memset(h3p[64:128, :], 0.0), writes=[R3])
                pg.op("dve", lambda e: e.tensor_copy(out=h3p[0:64, :].rearrange("p (b a) -> p b a", a=64),
                                                     in_=h3[:].rearrange("p (a b) -> p b a", b=64)),
                      reads=[R3], writes=[R3])
                gr = sb("f_gr", [128, 64 * 128], BF16, st)
                gi = sb("f_gi", [128, 64 * 128], BF16, st)
                pg.op("dve", lambda e: e.memset(w4[64:128, :], 0.0), writes=[R])
                pg.dma("gq", w4[0:64, :], hf_w4, writes=[R])
                pg.dma("sp", gr[:], c_gr, writes=[R])
                pg.dma("sp", gi[:], c_gi, writes=[R])
                dec = [sb("f_dec%d" % i, [128, 4, 512], F32, st) for i in range(2)]
                Rdec = [pg.res() for _ in range(2)]
                Hs = [sb("f_Hs%d" % i, [128, 1024], BF16, st) for i in range(2)]
                RHs = [pg.res() for _ in range(2)]
                stg = [sb("f_stg%d" % i, [128, 2, 1024], BF16, st) for i in range(2)]
                Rstg = [pg.res() for _ in range(2)]
                BdKv = BdK.rearrange("(part t) f c -> t f part c", part=2)

                def load_dec(g):
                    pg.dma("sp", dec[g % 2][:], c_decay[g * 4:(g + 1) * 4].rearrange("pr p c -> p pr c"),
                           writes=[Rdec[g % 2]])

                load_dec(0)

                def s0(t2i):
                    g = t2i // 8
                    if t2i % 8 == 4 and g + 1 < 8:
                        load_dec(g + 1)
                    if t2i % 2:
                        return
                    pr = t2i // 2
                    for hb in range(2):
                        bk = 2 * (pr % 2) + hb
                        pg.op("pe", lambda e: e.matmul(ps[bk][:], h3p[:, pr * 128:(pr + 1) * 128],
                                                       w4[:, hb * 512:(hb + 1) * 512],
                                                       start=True, stop=True), reads=[R3, R], writes=[Rps[bk]])

                def s1(t2i):
                    if t2i % 2:
                        return
                    g = t2i // 8
                    pr = t2i // 2
                    b = pr % 2
                    for hb in range(2):
                        bk = 2 * (pr % 2) + hb
                        pg.op("dve", lambda e: e.tensor_tensor(out=Hs[b][:, hb * 512:(hb + 1) * 512],
                                                               in0=ps[bk][:], in1=dec[g % 2][:, pr % 4, :],
                                                               op=ALU.mult),
                              reads=[Rps[bk], Rdec[g % 2]], writes=[RHs[b]])
                    if t2i == 0:
                        pg.op("dve", lambda e: e.memset(Hs[b][0:1, 512:1024], 0.0), writes=[RHs[b]])

                def s2(t2i):
                    b = (t2i // 2) % 2
                    for part, gm in enumerate((gr, gi)):
                        for hb in range(2):
                            bk = 4 + 2 * part + hb
                            pg.op("pe", lambda e: e.matmul(ps[bk][:], gm[:, t2i * 128:(t2i + 1) * 128],
                                                           Hs[b][:, hb * 512:(hb + 1) * 512], start=True, stop=True),
                                  reads=[R, RHs[b]], writes=[Rps[bk]])

                def s3(t2i):
                    b = t2i % 2
                    for part in range(2):
                        for hb in range(2):
                            bk = 4 + 2 * part + hb
                            if hb == 0:
                                pg.op("act", lambda e: e.activation(out=stg[b][:, part, 0:512], in_=ps[bk][:],
                                                                    func=AF.Copy), reads=[Rps[bk]], writes=[Rstg[b]])
                            else:
                                pg.op("dve", lambda e: e.tensor_copy(out=stg[b][:, part, 512:1024], in_=ps[bk][:]),
                                      reads=[Rps[bk]], writes=[Rstg[b]])
                    pg.dma("sp", BdKv[t2i], stg[b][:], reads=[Rstg[b]], writes=[Rd["BdK"]])

                skew(64, [s0, s1, s2, s3])
                pg.barrier()
            with contextlib.ExitStack() as st:
                kf = sb("k_kf", [128, 128], BF16, st)
                kb = sb("k_kb", [128, 128], BF16, st)
                R = pg.res()
                pg.dma("sp", kf[:], c_kf, writes=[R])
                pg.dma("sp", kb[:], c_kb, writes=[R])
                NG = 3
                Bt = [sb("k_Bt%d" % i, [128, 8, 1024], BF16, st) for i in range(NG)]
                RBt = [pg.res() for _ in range(NG)]
                Kst = [sb("k_st%d" % i, [128, 8, 512], BF16, st) for i in range(2)]
                RKst = [pg.res() for _ in range(2)]

                def load_group(g):
                    pg.dma("sp", Bt[g % NG][:], BdK[:, g * 8:(g + 1) * 8, :], reads=[Rd["BdK"]], writes=[RBt[g % NG]])

                load_group(0)

                def s0(n):
                    g, i = n // 8, n % 8
                    if i == 0 and g + 1 < 16:
                        load_group(g + 1)
                    b = g % NG
                    pt, Rp = ps[n % 4], Rps[n % 4]
                    pg.op("pe", lambda e: e.matmul(pt[:], kf[:], Bt[b][:, i, 0:512], start=True, stop=False),
                          reads=[R, RBt[b]], writes=[Rp], inc=False)
                    pg.op("pe", lambda e: e.matmul(pt[:], kb[:], Bt[b][:, i, 512:1024], start=False, stop=True),
                          reads=[R, RBt[b]], writes=[Rp])

                def s1(n):
                    g, i = n // 8, n % 8
                    kb_ = g % 2
                    pt, Rp = ps[n % 4], Rps[n % 4]
                    if i % 2 == 0:
                        pg.op("act", lambda e: e.activation(out=Kst[kb_][:, i, :], in_=pt[:], func=AF.Copy),
                              reads=[Rp], writes=[RKst[kb_]])
                    else:
                        pg.op("dve", lambda e: e.tensor_copy(out=Kst[kb_][:, i, :], in_=pt[:]),
                              reads=[Rp], writes=[RKst[kb_]])
                    if i == 7:
                        pg.dma("sp", KKd[:, g * 8:(g + 1) * 8, :], Kst[kb_][:], reads=[RKst[kb_]], writes=[Rd["KKd"]])

                skew(128, [s0, s1])
                pg.barrier()

        def phase_hyena():
            with contextlib.ExitStack() as st:
                vp = sb("h_vp", [128, 4, T], BF16, st)
                gr = sb("h_gr", [128, 64 * 128], BF16, st)
                gi = sb("h_gi", [128, 64 * 128], BF16, st)
                R = pg.res()
                for c in range(4):
                    pg.dma("sp", vp[:, c, :], vpT[c * 128:(c + 1) * 128, :], reads=[Rd["vpT"]], writes=[R])
                pg.dma("sp", gr[:], c_gr, writes=[R])
                pg.dma("sp", gi[:], c_gi, writes=[R])
                vpp = sb("h_vpp", [128, 4, T], BF16, st)
                for c in range(4):
                    eng_ = "dve" if c % 2 == 0 else "pool"
                    pg.op(eng_, lambda e: e.tensor_copy(out=vpp[:, c, :].rearrange("p (b a) -> p b a", a=64),
                                                        in_=vp[:, c, :].rearrange("p (a b) -> p b a", b=64)),
                          reads=[R], writes=[R])
                Xs = [sb("h_Xs%d" % i, [128, 512], BF16, st) for i in range(2)]
                RXs = [pg.res() for _ in range(2)]
                stg = [sb("h_stg%d" % i, [128, 2, 512], BF16, st) for i in range(2)]
                Rstg = [pg.res() for _ in range(2)]
                Bdv = Bd.rearrange("(part t) f c -> t f part c", part=2)

                def s0(t2i):
                    if t2i % 2:
                        return
                    pr = t2i // 2
                    bk = pr % 2
                    ptb = ps[bk].bitcast(BF16)
                    for c in range(4):
                        pg.op("pe", lambda e: e.transpose(ptb[:, c * 128:(c + 1) * 128],
                                                          vpp[:, c, pr * 128:(pr + 1) * 128], ident[:]),
                              reads=[R, R_const], writes=[Rps[bk]], inc=(c == 3))

                def s1(t2i):
                    if t2i % 2:
                        return
                    pr = t2i // 2
                    bk = pr % 2
                    ptb = ps[bk].bitcast(BF16)
                    pg.op("dve", lambda e: e.tensor_copy(out=Xs[pr % 2][:], in_=ptb[:, 0:512]), reads=[Rps[bk]],
                          writes=[RXs[pr % 2]])

                def s2(t2i):
                    xb = (t2i // 2) % 2
                    for part, gm in enumerate((gr, gi)):
                        bk = 2 + 2 * (t2i % 2) + part
                        pg.op("pe", lambda e: e.matmul(ps[bk][:], gm[:, t2i * 128:(t2i + 1) * 128], Xs[xb][:],
                                                       start=True, stop=True), reads=[R, RXs[xb]],
                              writes=[Rps[bk]])

                def s3(t2i):
                    b = t2i % 2
                    for part in range(2):
                        bk = 2 + 2 * (t2i % 2) + part
                        if part == 0:
                            pg.op("act", lambda e: e.activation(out=stg[b][:, part, :], in_=ps[bk][:], func=AF.Copy),
                                  reads=[Rps[bk]], writes=[Rstg[b]])
                        else:
                            pg.op("dve", lambda e: e.tensor_copy(out=stg[b][:, part, :], in_=ps[bk][:]),
                                  reads=[Rps[bk]], writes=[Rstg[b]])
                    pg.dma("sp", Bdv[t2i], stg[b][:], reads=[Rstg[b]], writes=[Rd["Bd"]])

                skew(64, [s0, s1, s2, s3])
                pg.barrier()
            with contextlib.ExitStack() as st:
                f2 = sb("h_f2", [128, 128], BF16, st)
                am = sb("h_a", [128, 128], BF16, st)
                R = pg.res()
                pg.dma("sp", f2[:], c_f2, writes=[R])
                pg.dma("sp", am[:], c_a, writes=[R])
                NG = 3
                Bt = [sb("h_Bt%d" % i, [128, 8, 512], BF16, st) for i in range(NG)]
                KKt = [sb("h_KK%d" % i, [128, 8, 512], BF16, st) for i in range(NG)]
                RBt = [pg.res() for _ in range(NG)]
                NP = 3
                Pm = [sb("h_P%d" % i, [128, 512], BF16, st) for i in range(NP)]
                RP = [pg.res() for _ in range(NP)]
                Zst = [sb("h_Zst%d" % i, [128, 8, 512], BF16, st) for i in range(2)]
                RZst = [pg.res() for _ in range(2)]

                def load_group(g):
                    b = g % NG
                    pg.dma("sp", Bt[b][:], Bd[:, g * 8:(g + 1) * 8, :], reads=[Rd["Bd"]], writes=[RBt[b]])
                    pg.dma("sp", KKt[b][:], KKd[:, g * 8:(g + 1) * 8, :], reads=[Rd["KKd"]], writes=[RBt[b]])

                load_group(0)
                def s0(n):
                    g, i = n // 8, n % 8
                    if i == 0 and g + 1 < 16:
                        load_group(g + 1)
                    b = g % NG
                    pt, Rp = ps[n % 2], Rps[n % 2]
                    pg.op("pe", lambda e: e.matmul(pt[:], f2[:], Bt[b][:, i, :], start=True, stop=True),
                          reads=[R, RBt[b]], writes=[Rp])

                def s1(n):
                    g, i = n // 8, n % 8
                    b = g % NG
                    pt, Rp = ps[n % 2], Rps[n % 2]
                    pg.op("dve", lambda e: e.tensor_tensor(out=Pm[n % NP][:], in0=pt[:], in1=KKt[b][:, i, :],
                                                           op=ALU.mult), reads=[Rp, RBt[b]], writes=[RP[n % NP]])

                def s2(n):
                    pt2, Rp2 = ps[2 + n % 2], Rps[2 + n % 2]
                    pg.op("pe", lambda e: e.matmul(pt2[:], am[:], Pm[n % NP][:], start=True, stop=True),
                          reads=[R, RP[n % NP]], writes=[Rp2])

                def s3(n):
                    g, i = n // 8, n % 8
                    zb = g % 2
                    pt2, Rp2 = ps[2 + n % 2], Rps[2 + n % 2]
                    pg.op("act", lambda e: e.activation(out=Zst[zb][:, i, :], in_=pt2[:], func=AF.Copy),
                          reads=[Rp2], writes=[RZst[zb]])
                    if i == 7:
                        pg.dma("sp", Zd[g * 8:(g + 1) * 8, :, :].rearrange("f z c -> z f c"), Zst[zb][:],
                               reads=[RZst[zb]], writes=[Rd["Zd"]])

                skew(128, [s0, s1, s2, s3])
                pg.barrier()
            with contextlib.ExitStack() as st:
                vp = sb("h2_vp", [128, 4, T], BF16, st)
                x0 = sb("h2_x0", [128, 4, T], BF16, st)
                yh = sb("h2_y", [128, 4, T], BF16, st)
                rbr = sb("h2_rbr", [128, 64 * 64], BF16, st)
                rbi = sb("h2_rbi", [128, 64 * 64], BF16, st)
                hyb = sb("h2_hyb", [128, 4], F32, st)
                R = pg.res()
                Ry = pg.res()
                for c in range(4):
                    pg.dma("sp", vp[:, c, :], vpT[c * 128:(c + 1) * 128, :], reads=[Rd["vpT"]], writes=[R])
                    pg.dma("sp", x0[:, c, :], x0T[c * 128:(c + 1) * 128, :], reads=[Rd["x0T"]], writes=[R])
                pg.dma("sp", rbr[:], c_rbr, writes=[R])
                pg.dma("sp", rbi[:], c_rbi, writes=[R])
                pg.dma("sp", hyb[:], hyb_col, writes=[R])
                Zt = [sb("h2_Zt%d" % i, [128, 2, 8, 512], BF16, st) for i in range(2)]
                RZt = [pg.res() for _ in range(2)]
                tmp = [sb("h2_tmp%d" % i, [128, 8, 64], F32, st) for i in range(2)]
                Rtmp = [pg.res() for _ in range(2)]
                Zdv = Zd.rearrange("f (z t) c -> f z t c", z=2)
                nb = 0

                def load_zt(tg):
                    pg.dma("sp", Zt[tg % 2][:], Zdv[:, :, tg * 8:(tg + 1) * 8, :], reads=[Rd["Zd"]],
                           writes=[RZt[tg % 2]])

                load_zt(0)
                for tg in range(8):
                    b = tg % 2
                    if tg + 1 < 8:
                        load_zt(tg + 1)
                    for c in range(4):
                        tb = nb % 2
                        nb += 1
                        pt, Rp = next_ps()
                        for j in range(8):
                            t2i = tg * 8 + j
                            for z, rb in enumerate((rbr, rbi)):
                                pg.op("pe", lambda e: e.matmul(pt[:, j * 64:(j + 1) * 64],
                                                               Zt[b][:, z, j, c * 128:(c + 1) * 128],
                                                               rb[:, t2i * 64:(t2i + 1) * 64],
                                                               start=(z == 0), stop=(z == 1)),
                                      reads=[R, RZt[b]], writes=[Rp], inc=(j == 7 and z == 1))
                        vview = vp[:, c, :].rearrange("p (a b) -> p b a", b=64)[:, tg * 8:(tg + 1) * 8, :]
                        xview = x0[:, c, :].rearrange("p (a b) -> p b a", b=64)[:, tg * 8:(tg + 1) * 8, :]
                        yview = yh[:, c, :].rearrange("p (a b) -> p b a", b=64)[:, tg * 8:(tg + 1) * 8, :]
                        pview = pt[:].rearrange("p (j a) -> p j a", j=8)
                        pg.op("dve", lambda e: e.scalar_tensor_tensor(out=tmp[tb][:], in0=vview, scalar=hyb[:, c:c + 1],
                                                                      in1=pview, op0=ALU.mult, op1=ALU.add),
                              reads=[R, Rp], writes=[Rtmp[tb]])
                        pg.op("pool", lambda e: e.tensor_tensor(out=yview, in0=tmp[tb][:], in1=xview, op=ALU.mult),
                              reads=[Rtmp[tb], R], writes=[Ry])
                for c in range(4):
                    pg.dma("sp", yT[c * 128:(c + 1) * 128, :], yh[:, c, :], reads=[Ry], writes=[Rd["yT"]])
                pg.barrier()

        def phase_attn():
            with contextlib.ExitStack() as st:
                NBUF = 2
                qj = [sb("a_q%d" % i, [128, T], BF16, st) for i in range(NBUF)]
                kj = [sb("a_k%d" % i, [128, T], BF16, st) for i in range(NBUF)]
                kcj = [sb("a_kc%d" % i, [128, 256], BF16, st) for i in range(NBUF)]
                vE = [sb("a_vE%d" % i, [128, 32, 128], BF16, st) for i in range(NBUF)]
                vO = [sb("a_vO%d" % i, [128, 31, 128], BF16, st) for i in range(NBUF)]
                vC = [sb("a_vC%d" % i, [128, 2, 128], BF16, st) for i in range(NBUF)]
                t2t = [sb("a_t2%d" % i, [128, 2, 14 * 64], F32, st) for i in range(NBUF)]
                yn = [sb("a_yn%d" % i, [128, T], BF16, st) for i in range(NBUF)]
                Rin = [pg.res() for _ in range(NBUF)]
                Ryn = [pg.res() for _ in range(NBUF)]
                ones = sb("a_ones", [128, 64], BF16, st)
                Ro = pg.res()
                pg.op("dve", lambda e: e.memset(ones[:], 1.0), writes=[Ro])
                NS = 3
                Ein = [sb("a_E%d" % s_, [128, 2, 256], F32, st) for s_ in range(NS)]
                PT = [sb("a_P%d" % s_, [128, 2, 384], BF16, st) for s_ in range(NS)]
                rec = [sb("a_rec%d" % s_, [128, 64], F32, st) for s_ in range(NS)]
                RE = [pg.res() for _ in range(NS)]
                RPT = [pg.res() for _ in range(NS)]
                Rrec = [pg.res() for _ in range(NS)]
                Rss = [pg.res() for _ in range(3)]

                def load_pair(j):
                    pb = j % NBUF
                    Ri = Rin[pb]
                    pg.dma("sp", qj[pb][:], qT[j * 128:(j + 1) * 128, :], reads=[Rd["qT"]], writes=[Ri])
                    pg.dma("sp", kj[pb][:], kT[j * 128:(j + 1) * 128, :], reads=[Rd["kT"]], writes=[Ri])
                    pg.dma("sp", kcj[pb][:], kcT[j * 128:(j + 1) * 128, :], reads=[Rd["kcT"]], writes=[Ri])
                    cs = slice(j * 128, (j + 1) * 128)
                    pg.dma("sp", vE[pb][:], v_d[:, cs].rearrange("(i p) c -> p i c", p=128),
                           reads=[Rd["v_d"]], writes=[Ri])
                    pg.dma("sp", vO[pb][:], v_d[64:64 + 31 * 128, cs].rearrange("(i p) c -> p i c", p=128),
                           reads=[Rd["v_d"]], writes=[Ri])
                    pg.dma("sp", vC[pb][:], vc_d[:, cs].rearrange("(i p) c -> p i c", p=128),
                           reads=[Rd["vc_d"]], writes=[Ri])
                    for hh in range(2):
                        pg.dma("sp", t2t[pb][:, hh, :], rpb_t2[2 * j + hh], writes=[Ri])

                load_pair(0)
                for j in range(4):
                    pb = j % NBUF
                    Ri = Rin[pb]
                    if j + 1 < 4:
                        load_pair(j + 1)

                    def geo(r):
                        rs = min(max(r - 4, 0), 56)
                        return rs, rs - r + 7, 64 * rs

                    def s0(r):
                        rs, dr0, tb0 = geo(r)
                        s2_ = r % 3
                        for hh in range(2):
                            hp = slice(64 * hh, 64 * hh + 64)
                            pt = ps[2 * s2_ + hh]
                            for jj in range(6):
                                if jj < 4:
                                    lt = kj[pb][hp, tb0 + 128 * jj: tb0 + 128 * (jj + 1)]
                                else:
                                    lt = kcj[pb][hp, 128 * (jj - 4):128 * (jj - 3)]
                                pg.op("pe", lambda e: e.matmul(pt[:, jj * 64:(jj + 1) * 64], lt,
                                                               qj[pb][hp, 64 * r:64 * r + 64], start=True, stop=True),
                                      reads=[Ri], writes=[Rss[s2_]], inc=(hh == 1 and jj == 5))

                    def s1(r):
                        rs, dr0, tb0 = geo(r)
                        s2_ = r % 3
                        s3_ = r % NS
                        pS2 = psall[:, (2 * s2_) * 512:(2 * s2_ + 2) * 512].rearrange("p (h c) -> p h c", h=2)
                        tview = t2t[pb][:].rearrange("p h (m q) -> p h m q", q=64)[:, :, dr0:dr0 + 7:2, :]
                        pg.op("dve", lambda e: e.tensor_tensor(
                            out=Ein[s3_][:].rearrange("p h (m q) -> p h m q", q=64),
                            in0=pS2[:, :, 0:256].rearrange("p h (m q) -> p h m q", q=64), in1=tview, op=ALU.add),
                            reads=[Rss[s2_], Ri], writes=[RE[s3_]])

                    def s1b(r):
                        s2_ = r % 3
                        s3_ = r % NS
                        pS2 = psall[:, (2 * s2_) * 512:(2 * s2_ + 2) * 512].rearrange("p (h c) -> p h c", h=2)
                        pg.op("act", lambda e: e.activation(out=PT[s3_][:, :, 0:256], in_=Ein[s3_][:], func=AF.Exp),
                              reads=[RE[s3_]], writes=[RPT[s3_]])
                        pg.op("act", lambda e: e.activation(out=PT[s3_][:, :, 256:384], in_=pS2[:, :, 256:384],
                                                            func=AF.Exp), reads=[Rss[s2_]], writes=[RPT[s3_]])

                    def s2(r):
                        rs, dr0, tb0 = geo(r)
                        s3_ = r % NS
                        if rs % 2 == 0:
                            vsrc, i0_ = vE[pb], rs // 2
                        else:
                            vsrc, i0_ = vO[pb], (rs - 1) // 2
                        pO, RpO = ps[6 + r % 2], Rps[6 + r % 2]
                        for hh in range(2):
                            hp = slice(64 * hh, 64 * hh + 64)
                            for jj in range(6):
                                if jj < 4:
                                    vt = vsrc[:, i0_ + jj, 64 * hh:64 * hh + 64]
                                else:
                                    vt = vC[pb][:, jj - 4, 64 * hh:64 * hh + 64]
                                pg.op("pe", lambda e: e.matmul(pO[hp, 0:64], vt, PT[s3_][:, hh, jj * 64:(jj + 1) * 64],
                                                               start=(jj == 0), stop=(jj == 5)),
                                      reads=[Ri, RPT[s3_]], writes=[RpO], inc=False)
                            for jj in range(6):
                                pg.op("pe", lambda e: e.matmul(pO[hp, 64:128], ones[:],
                                                               PT[s3_][:, hh, jj * 64:(jj + 1) * 64],
                                                               start=(jj == 0), stop=(jj == 5)),
                                      reads=[Ro, RPT[s3_]], writes=[RpO], inc=(hh == 1 and jj == 5))

                    def s3(r):
                        s3_ = r % NS
                        pO, RpO = ps[6 + r % 2], Rps[6 + r % 2]
                        pg.op("dve", lambda e: e.reciprocal(out=rec[s3_][:], in_=pO[:, 64:128]), reads=[RpO],
                              writes=[Rrec[s3_]])
                        pg.op("dve", lambda e: e.tensor_tensor(out=yn[pb][:, 64 * r:64 * r + 64], in0=pO[:, 0:64],
                                                               in1=rec[s3_][:], op=ALU.mult),
                              reads=[RpO, Rrec[s3_]], writes=[Ryn[pb]])

                    skew(64, [s0, s1, s1b, s2, s3])
                    pg.dma("sp", yT[512 + j * 128:512 + (j + 1) * 128, :], yn[pb][:], reads=[Ryn[pb]],
                           writes=[Rd["yT"]])
                pg.barrier()

        def phase_outproj0():
            with contextlib.ExitStack() as st:
                yS = sb("o_y", [128, KC, T], BF16, st)
                wo = sb("o_w", [128, KC, D], BF16, st)
                R = pg.res()
                for kc in range(KC):
                    pg.dma("sp", yS[:, kc, :], yT[kc * 128:(kc + 1) * 128, :], reads=[Rd["yT"]], writes=[R])
                Rw = pg.res()
                for hf in range(2):
                    pg.dma("gq", wo[:, :, hf * 512:(hf + 1) * 512],
                           w_out.rearrange("(kc p) n -> p kc n", p=128)[:, :, hf * 512:(hf + 1) * 512], writes=[Rw])
                ru = RU(st, 0, 2, "o_ru")
                Rx = pg.res()
                hv = {}

                def s0(i):
                    ru.load(i, x, Rx)
                    halves = []
                    for hf in range(2):
                        pt, Rp = next_ps()
                        for kc in range(KC):
                            pg.op("pe", lambda e: e.matmul(pt[:], yS[:, kc, i * 128:(i + 1) * 128],
                                                           wo[:, kc, hf * 512:(hf + 1) * 512],
                                                           start=(kc == 0), stop=(kc == KC - 1)),
                                  reads=[R, Rw], writes=[Rp], inc=(kc == KC - 1))
                        halves.append((pt, Rp))
                    hv[i] = halves

                def s1(i):
                    halves = hv.pop(i)
                    ru.scale(i, [h[0][:] for h in halves], [h[1] for h in halves])

                def s2(i):
                    ru.finish(i, hbuf, Rd["hbuf"])

                skew(32, [s0, s1, s2])
                pg.barrier()

        def phase_ffn_up(l):
            hnT = H["hnT"]
            with contextlib.ExitStack() as st:
                fw = sb("f1_fw", [128, 44, 3], F32, st)
                fbv = sb("f1_fb", [128, 44], F32, st)
                Rc = pg.res()
                pg.dma("sp", fw[:], ffn_wdw_col[l], writes=[Rc])
                pg.dma("sp", fbv[:], ffn_bdw_col[l], writes=[Rc])
                NW = 3
                wg = [sb("f1_wg%d" % i, [128, KC, 128], BF16, st) for i in range(NW)]
                wu = [sb("f1_wu%d" % i, [128, KC, 128], BF16, st) for i in range(NW)]
                Rw = [pg.res() for _ in range(NW)]
                stg = [sb("f1_stg%d" % i, [128, T], BF16, st) for i in range(2)]
                Rstg = [pg.res() for _ in range(2)]
                NB = 3
                t0 = [sb("f1_t0%d" % i, [128, 512], F32, st) for i in range(2 * NB)]
                t1 = [sb("f1_t1%d" % i, [128, 512], F32, st) for i in range(2 * NB)]
                gc = [sb("f1_gc%d" % i, [128, 512], F32, st) for i in range(NB)]
                uc = [sb("f1_uc%d" % i, [128, 512], F32, st) for i in range(NB)]
                sg = [sb("f1_sg%d" % i, [128, 512], F32, st) for i in range(NB)]
                Rt = [pg.res() for _ in range(2 * NB)]
                Rgc = [pg.res() for _ in range(NB)]
                Ruc = [pg.res() for _ in range(NB)]
                Rsg = [pg.res() for _ in range(NB)]
                NT = len(TT_CONV)
                starts = [sum(TT_CONV[:k]) for k in range(NT)]

                def load_weights(j):
                    b = j % NW
                    load_w(wg[b], ffn_w_up[l], j * 128, 128, Rw[b])
                    load_w(wu[b], ffn_w_up[l], DFF + j * 128, 128, Rw[b])

                load_weights(0)
                pv = {}

                def s0(i):
                    j, k = i // NT, i % NT
                    b = j % NW
                    if k == 0 and j + 1 < NFF:
                        load_weights(j + 1)
                    n, s_ = TT_CONV[k], starts[k]
                    pg_, Rpg = next_ps()
                    for kc in range(KC):
                        pg.op("pe", lambda e: e.matmul(pg_[:, 0:n + 2], wg[b][:, kc, :], hnT[:, kc, s_:s_ + n + 2],
                                                       start=(kc == 0), stop=(kc == KC - 1)),
                              reads=[Rw[b], R_hnT], writes=[Rpg], inc=(kc == KC - 1))
                    pu_, Rpu = next_ps()
                    for kc in range(KC):
                        pg.op("pe", lambda e: e.matmul(pu_[:, 0:n + 2], wu[b][:, kc, :], hnT[:, kc, s_:s_ + n + 2],
                                                       start=(kc == 0), stop=(kc == KC - 1)),
                              reads=[Rw[b], R_hnT], writes=[Rpu], inc=(kc == KC - 1))
                    pv[i] = (pg_, Rpg, pu_, Rpu)

                def s1(i):
                    j, k = i // NT, i % NT
                    n = TT_CONV[k]
                    tb = i % NB
                    pg_, Rpg, pu_, Rpu = pv.pop(i)
                    conv3_evac(pg_, Rpg, n, fw[:, j, :], fbv[:, j:j + 1], Rc, t0[2 * tb], t1[2 * tb],
                               gc[tb][:, 0:n], Rt[2 * tb], Rgc[tb])
                    conv3_evac(pu_, Rpu, n, fw[:, NFF + j, :], fbv[:, NFF + j:NFF + j + 1], Rc, t0[2 * tb + 1],
                               t1[2 * tb + 1], uc[tb][:, 0:n], Rt[2 * tb + 1], Ruc[tb])

                def s2(i):
                    j, k = i // NT, i % NT
                    n, s_ = TT_CONV[k], starts[k]
                    tb = i % NB
                    sbuf_i = j % 2
                    pg.op("act", lambda e: e.activation(out=sg[tb][:, 0:n], in_=gc[tb][:, 0:n], func=AF.Silu),
                          reads=[Rgc[tb]], writes=[Rsg[tb]])
                    pg.op("pool", lambda e: e.tensor_tensor(out=stg[sbuf_i][:, s_:s_ + n], in0=sg[tb][:, 0:n],
                                                            in1=uc[tb][:, 0:n], op=ALU.mult),
                          reads=[Rsg[tb], Ruc[tb]], writes=[Rstg[sbuf_i]])
                    if k == NT - 1:
                        pg.dma("sp", actT[j * 128:(j + 1) * 128, :], stg[sbuf_i][:], reads=[Rstg[sbuf_i]],
                               writes=[Rd["actT"]])

                skew(NFF * NT, [s0, s1, s2])
                pg.barrier()

        def phase_ffn_down(l, hdst, Rdst):
            with contextlib.ExitStack() as st:
                wd = sb("f2_wd", [128, NFF, D], BF16, st)
                Rw = pg.res()
                wsrc = ffn_w_down[l].rearrange("(j p) n -> p j n", p=128)
                for j0 in range(0, NFF, 2):
                    pg.dma("gq", wd[:, j0:j0 + 2, :], wsrc[:, j0:j0 + 2, :], writes=[Rw])
                aS = [sb("f2_a%d" % i, [128, NFF, 512], BF16, st) for i in range(2)]
                Ra = [pg.res() for _ in range(2)]
                ru = RU(st, l, 5, "f2_ru")
                asrc = actT.rearrange("(j p) t -> p j t", p=128)
                hv = {}
                R_hload = pg.res()

                def load_slab(sl):
                    pg.dma("sp", aS[sl % 2][:], asrc[:, :, sl * 512:(sl + 1) * 512], reads=[Rd["actT"]],
                           writes=[Ra[sl % 2]])

                load_slab(0)

                def s0(i):
                    sl, sub = i // 4, i % 4
                    b = sl % 2
                    if sub == 0 and sl + 1 < 8:
                        load_slab(sl + 1)
                    ru.load(i, hbuf, R_hload)
                    halves = []
                    for hf in range(2):
                        pt, Rp = next_ps()
                        for j in range(NFF):
                            pg.op("pe", lambda e: e.matmul(pt[:], aS[b][:, j, sub * 128:(sub + 1) * 128],
                                                           wd[:, j, hf * 512:(hf + 1) * 512],
                                                           start=(j == 0), stop=(j == NFF - 1)),
                                  reads=[Ra[b], Rw], writes=[Rp], inc=(j == NFF - 1))
                        halves.append((pt, Rp))
                    hv[i] = halves

                def s1(i):
                    halves = hv.pop(i)
                    ru.scale(i, [h[0][:] for h in halves], [h[1] for h in halves])

                def s2(i):
                    ru.finish(i, hdst, Rdst)

                skew(32, [s0, s1, s2])
                pg.barrier()

        def phase_conformer_a():
            hnT = H["hnT"]
            with contextlib.ExitStack() as st:
                cb1 = sb("c_b1", [128, 16], F32, st)
                cw = sb("c_w", [128, 8, 31], F32, st)
                cbd = sb("c_bd", [128, 8], F32, st)
                Rc = pg.res()
                pg.dma("sp", cb1[:], cf_b1_col, writes=[Rc])
                pg.dma("sp", cw[:], cf_wdw_col, writes=[Rc])
                pg.dma("sp", cbd[:], cf_bdw_col, writes=[Rc])
                wa = [sb("c_wa%d" % i, [128, KC, 128], BF16, st) for i in range(2)]
                wgt = [sb("c_wg%d" % i, [128, KC, 128], BF16, st) for i in range(2)]
                Rw = [pg.res() for _ in range(2)]
                uT = [sb("c_uT%d" % i, [128, T + 30], BF16, st) for i in range(2)]
                RuT = [pg.res() for _ in range(2)]
                dg = [sb("c_dg%d" % i, [128, 31, 128], BF16, st) for i in range(2)]
                Rdg = [pg.res() for _ in range(2)]
                sig = [sb("c_sig%d" % i, [128, 512], F32, st) for i in range(2)]
                Rsig = [pg.res() for _ in range(2)]
                u2s = [sb("c_u2s%d" % i, [128, T], F32, st) for i in range(2)]
                Ru2 = [pg.res() for _ in range(2)]
                for b in range(2):
                    pg.op("dve", lambda e: e.memset(uT[b][:, 0:15], 0.0), writes=[RuT[b]])
                    pg.op("dve", lambda e: e.memset(uT[b][:, T + 15:T + 30], 0.0), writes=[RuT[b]])
                nblk = 0

                def prep(j):
                    b = j % 2
                    load_w(wa[b], cf_w_pw1, j * 128, 128, Rw[b])
                    load_w(wgt[b], cf_w_pw1, D + j * 128, 128, Rw[b])
                    for tap in range(31):
                        pg.op("pool", lambda e: e.tensor_scalar(out=dg[b][:, tap, :], in0=ident[:],
                                                                scalar1=cw[:, j, tap:tap + 1], scalar2=1.0,
                                                                op0=ALU.mult, op1=ALU.mult),
                              reads=[R_const, Rc], writes=[Rdg[b]])

                prep(0)
                for j in range(8):
                    b = j % 2
                    if j + 1 < 8:
                        prep(j + 1)
                    for ti in range(8):
                        tb = nblk % 2
                        nblk += 1
                        cs = slice(1 + ti * 512, 1 + (ti + 1) * 512)
                        pa, Rpa = next_ps()
                        for kc in range(KC):
                            pg.op("pe", lambda e: e.matmul(pa[:], wa[b][:, kc, :], hnT[:, kc, cs],
                                                           start=(kc == 0), stop=(kc == KC - 1)),
                                  reads=[Rw[b], R_hnT], writes=[Rpa], inc=(kc == KC - 1))
                        pgt, Rpg = next_ps()
                        for kc in range(KC):
                            pg.op("pe", lambda e: e.matmul(pgt[:], wgt[b][:, kc, :], hnT[:, kc, cs],
                                                           start=(kc == 0), stop=(kc == KC - 1)),
                                  reads=[Rw[b], R_hnT], writes=[Rpg], inc=(kc == KC - 1))
                        pg.op("act", lambda e: e.activation(out=sig[tb][:], in_=pgt[:], func=AF.Sigmoid,
                                                            bias=cb1[:, 8 + j:9 + j]),
                              reads=[Rpg, Rc], writes=[Rsig[tb]])
                        pg.op("dve", lambda e: e.scalar_tensor_tensor(
                            out=uT[b][:, 15 + ti * 512:15 + (ti + 1) * 512], in0=pa[:], scalar=cb1[:, j:j + 1],
                            in1=sig[tb][:], op0=ALU.add, op1=ALU.mult),
                            reads=[Rpa, Rc, Rsig[tb]], writes=[RuT[b]])
                    for ti in range(8):
                        pt, Rp = next_ps()
                        for tap in range(31):
                            pg.op("pe", lambda e: e.matmul(pt[:], dg[b][:, tap, :],
                                                           uT[b][:, ti * 512 + tap: ti * 512 + tap + 512],
                                                           start=(tap == 0), stop=(tap == 30)),
                                  reads=[Rdg[b], RuT[b]], writes=[Rp], inc=(tap == 30))
                        pg.op("act", lambda e: e.activation(out=u2s[b][:, ti * 512:(ti + 1) * 512], in_=pt[:],
                                                            func=AF.Identity, bias=cbd[:, j:j + 1]),
                              reads=[Rp, Rc], writes=[Ru2[b]])
                    pg.dma("sp", u2T[j * 128:(j + 1) * 128, :], u2s[b][:], reads=[Ru2[b]], writes=[Rd["u2T"]])
                pg.barrier()

        def phase_conformer_b():
            with contextlib.ExitStack() as st:
                lng = sb("c2_g", [128, 8], F32, st)
                lnb = sb("c2_b", [128, 8], F32, st)
                onesf = sb("c2_ones", [128, 128], F32, st)
                bp2 = sb("c2_bp2", [128, D], F32, st)
                wp2 = sb("c2_w", [128, KC, D], BF16, st)
                Rc = pg.res()
                pg.dma("sp", lng[:], cf_lng_col, writes=[Rc])
                pg.dma("sp", lnb[:], cf_lnb_col, writes=[Rc])
                pg.dma("sp", bp2[:], cf_b_pw2.partition_broadcast(128)[:, 0, :], writes=[Rc])
                pg.op("dve", lambda e: e.memset(onesf[:], 1.0 / D), writes=[Rc])
                for hf in range(2):
                    pg.dma("gq", wp2[:, :, hf * 512:(hf + 1) * 512],
                           cf_w_pw2.rearrange("(kc p) n -> p kc n", p=128)[:, :, hf * 512:(hf + 1) * 512],
                           writes=[Rc])
                u2 = [sb("c2_u2%d" % i, [128, KC, 512], F32, st) for i in range(2)]
                Ru2 = [pg.res() for _ in range(2)]
                sq = sb("c2_sq", [128, KC, 512], F32, st)
                Rsq = pg.res()
                mean = sb("c2_mean", [128, 512], F32, st)
                m2 = sb("c2_m2", [128, 512], F32, st)
                var = sb("c2_var", [128, 512], F32, st)
                rstd = sb("c2_rstd", [128, 512], F32, st)
                Rst = pg.res()
                d1 = [sb("c2_d1%d" % i, [128, 512], F32, st) for i in range(2)]
                d2 = [sb("c2_d2%d" % i, [128, 512], F32, st) for i in range(2)]
                Rd1 = [pg.res() for _ in range(2)]
                Rd2 = [pg.res() for _ in range(2)]
                u3 = [sb("c2_u3%d" % i, [128, KC, 512], BF16, st) for i in range(2)]
                Ru3 = [pg.res() for _ in range(2)]
                ysb = [sb("c2_y%d" % i, [128, D], F32, st) for i in range(2)]
                Ry = [pg.res() for _ in range(2)]
                ru = RU(st, 1, 2, "c2_ru")
                R_hload = pg.res()
                u2src = u2T.rearrange("(kc p) t -> p kc t", p=128)
                nd = 0
                ny = 0
                def load_u2(ti):
                    pg.dma("sp", u2[ti % 2][:], u2src[:, :, ti * 512:(ti + 1) * 512], reads=[Rd["u2T"]],
                           writes=[Ru2[ti % 2]])

                load_u2(0)
                cnt = {"nd": 0, "ny": 0}

                def sA(ti):
                    b = ti % 2
                    if ti + 1 < 8:
                        load_u2(ti + 1)
                    pg.op("act", lambda e: e.activation(out=sq[:], in_=u2[b][:], func=AF.Square),
                          reads=[Ru2[b]], writes=[Rsq])
                    pm, Rpm = next_ps()
                    for kc in range(KC):
                        pg.op("pe", lambda e: e.matmul(pm[:], onesf[:], u2[b][:, kc, :], start=(kc == 0),
                                                       stop=(kc == KC - 1)),
                              reads=[Rc, Ru2[b]], writes=[Rpm], inc=(kc == KC - 1))
                    pq, Rpq = next_ps()
                    for kc in range(KC):
                        pg.op("pe", lambda e: e.matmul(pq[:], onesf[:], sq[:, kc, :], start=(kc == 0),
                                                       stop=(kc == KC - 1)),
                              reads=[Rc, Rsq], writes=[Rpq], inc=(kc == KC - 1))
                    pg.op("act", lambda e: e.activation(out=mean[:], in_=pm[:], func=AF.Copy), reads=[Rpm],
                          writes=[Rst])
                    pg.op("dve", lambda e: e.tensor_tensor(out=m2[:], in0=mean[:], in1=mean[:], op=ALU.mult),
                          reads=[Rst], writes=[Rst])
                    pg.op("dve", lambda e: e.tensor_tensor(out=var[:], in0=pq[:], in1=m2[:], op=ALU.subtract),
                          reads=[Rpq, Rst], writes=[Rst])
                    pg.op("act", lambda e: e.activation(out=var[:], in_=var[:], func=AF.Ln, bias=epsc[:, 0:1]),
                          reads=[Rst, R_const], writes=[Rst])
                    pg.op("act", lambda e: e.activation(out=rstd[:], in_=var[:], func=AF.Exp, scale=-0.5),
                          reads=[Rst], writes=[Rst])
                    for kc in range(KC):
                        db = cnt["nd"] % 2
                        cnt["nd"] += 1
                        pg.op("dve", lambda e: e.tensor_tensor(out=d1[db][:], in0=u2[b][:, kc, :], in1=mean[:],
                                                               op=ALU.subtract),
                              reads=[Ru2[b], Rst], writes=[Rd1[db]])
                        pg.op("pool", lambda e: e.tensor_tensor(out=d2[db][:], in0=d1[db][:], in1=rstd[:],
                                                                op=ALU.mult),
                              reads=[Rd1[db], Rst], writes=[Rd2[db]])
                        pg.op("act", lambda e: e.activation(out=u3[b][:, kc, :], in_=d2[db][:], func=AF.Silu,
                                                            bias=lnb[:, kc:kc + 1], scale=lng[:, kc:kc + 1]),
                              reads=[Rd2[db], Rc], writes=[Ru3[b]])

                def sB(ti):
                    b = ti % 2
                    for sub in range(4):
                        i = ti * 4 + sub
                        yb = cnt["ny"] % 2
                        cnt["ny"] += 1
                        for hf in range(2):
                            pt, Rp = next_ps()
                            for kc in range(KC):
                                pg.op("pe", lambda e: e.matmul(pt[:], u3[b][:, kc, sub * 128:(sub + 1) * 128],
                                                               wp2[:, kc, hf * 512:(hf + 1) * 512],
                                                               start=(kc == 0), stop=(kc == KC - 1)),
                                      reads=[Ru3[b], Rc], writes=[Rp], inc=(kc == KC - 1))
                            pg.op("dve", lambda e: e.tensor_tensor(out=ysb[yb][:, hf * 512:(hf + 1) * 512], in0=pt[:],
                                                                   in1=bp2[:, hf * 512:(hf + 1) * 512], op=ALU.add),
                                  reads=[Rp, Rc], writes=[Ry[yb]])
                        ru.load(i, hbuf, R_hload)
                        ru.scale(i, [ysb[yb][:, 0:512], ysb[yb][:, 512:1024]], [Ry[yb], Ry[yb]])
                        ru.finish(i, hbuf, Rd["hbuf"])

                skew(8, [sA, sB], reverse=False)
                pg.barrier()

        hn_stack = contextlib.ExitStack()

        def hn_on():
            hn_open(hn_stack)

        def hn_off():
            hn_stack.close()

        steps = [
            ("mod0", lambda: phase_mod(0)),
            ("filters", phase_filters),
            ("hn_on", hn_on),
            ("pn0", lambda: phase_prenorm(x, pg.res(), T, 0, 0, 1, H["hnT"], R_hnT, 1)),
            ("proj0", phase_proj0),
            ("hn_off", hn_off),
            ("hyena", phase_hyena),
            ("attn", phase_attn),
            ("outproj0", phase_outproj0),
            ("hn_on", hn_on),
            ("pn0f", lambda: phase_prenorm(hbuf, Rd["hbuf"], T, 0, 3, 4, H["hnT"], R_hnT, 1)),
            ("ffn0u", lambda: phase_ffn_up(0)),
            ("hn_off", hn_off),
            ("ffn0d", lambda: phase_ffn_down(0, hbuf, Rd["hbuf"])),
            ("mod1", lambda: phase_mod(1)),
            ("hn_on", hn_on),
            ("pn1", lambda: phase_prenorm(hbuf, Rd["hbuf"], T, 1, 0, 1, H["hnT"], R_hnT, 1)),
            ("confa", phase_conformer_a),
            ("hn_off", hn_off),
            ("confb", phase_conformer_b),
            ("hn_on", hn_on),
            ("pn1f", lambda: phase_prenorm(hbuf, Rd["hbuf"], T, 1, 3, 4, H["hnT"], R_hnT, 1)),
            ("ffn1u", lambda: phase_ffn_up(1)),
            ("hn_off", hn_off),
            ("ffn1d", lambda: phase_ffn_down(1, out, Rd["out"])),
        ]
        skip = set(skip_steps or ())
        for name, fn in steps:
            if name in skip:
                continue
            fn()
            if stop is not None and name == stop:
                break
        pg.barrier()
        hn_stack.close()
    return nc, dbg_names


def make_in_maps(inputs):
    f32 = lambda a: np.ascontiguousarray(np.asarray(a, np.float32))
    consts = make_consts()
    shared = {
        "w_mod": f32(inputs["w_mod"]), "b_mod": f32(inputs["b_mod"]),
        "gvec": f32(np.stack([inputs["g_mix_pre"], inputs["g_mix_post"], inputs["g_ffn_pre"],
                              inputs["g_ffn_post"]], axis=1)),
        "w_in": f32(inputs["w_in"][0]), "w_out": f32(inputs["w_out"][0]),
        "hsw_col": col(inputs["hy_short_w"][0], 12), "hsb_col": col(inputs["hy_short_b"][0], 12),
        "hf_w1": f32(inputs["hy_f_w1"][0]), "hf_w2": f32(inputs["hy_f_w2"][0]), "hf_w3": f32(inputs["hy_f_w3"][0]),
        "hf_w4": f32(inputs["hy_f_w4"][0]),
        "hf_b_col": f32(np.stack([inputs["hy_f_b1"][0], inputs["hy_f_b2"][0], inputs["hy_f_b3"][0]], axis=1)),
        "hf_freq_col": f32(np.asarray(inputs["hy_f_freq"][0]).T),
        "hyb_col": col(inputs["hy_bias"][0], 4),
        "rpb_t2": make_rpb_table(np.asarray(inputs["na_rpb"][0], np.float32)),
        "cf_w_pw1": f32(inputs["cf_w_pw1"][0]), "cf_b1_col": col(inputs["cf_b_pw1"][0], 16),
        "cf_wdw_col": col(inputs["cf_w_dw"][0], 8), "cf_bdw_col": col(inputs["cf_b_dw"][0], 8),
        "cf_lng_col": col(inputs["cf_ln_g"][0], 8), "cf_lnb_col": col(inputs["cf_ln_b"][0], 8),
        "cf_w_pw2": f32(inputs["cf_w_pw2"][0]), "cf_b_pw2": f32(np.asarray(inputs["cf_b_pw2"][0])[None, :]),
        "ffn_w_up": f32(inputs["ffn_w_up"]),
        "ffn_wdw_col": np.stack([col(inputs["ffn_w_dw"][l], 44) for l in range(2)]),
        "ffn_bdw_col": np.stack([col(inputs["ffn_b_dw"][l], 44) for l in range(2)]),
        "ffn_w_down": f32(inputs["ffn_w_down"]),
    }
    shared.update(consts)
    maps = []
    for b in range(NCORES):
        m = dict(shared)
        m["x"] = f32(inputs["x"][b])
        m["ctx"] = f32(inputs["ctx"][b])
        cc = np.stack([np.asarray(inputs["c"][b], np.float32), np.asarray(inputs["c_ctx"], np.float32)], 0)
        m["ccol"] = np.ascontiguousarray(np.transpose(cc.reshape(2, 8, 128), (2, 0, 1)))
        maps.append(m)
    return maps


def kernel(**inputs):
    inputs = {k: np.asarray(v) for k, v in inputs.items()}
    nc, _ = build()
    maps = make_in_maps(inputs)
    for m in maps:
        for k, (shp, dt) in INPUT_SPECS.items():
            assert m[k].shape == shp and m[k].dtype == dt, (k, m[k].shape, shp, m[k].dtype, dt)
    res = run_bass_kernel_spmd(nc, maps, core_ids=list(range(NCORES)))
    outs = [np.asarray(res.results[b]["out"], np.float32) for b in range(NCORES)]
    return np.stack(outs, 0)
```

```python
import contextlib
import math
import numpy as np
import ml_dtypes
import concourse.bass as bass
import concourse.mybir as mybir
from concourse.bass_utils import run_bass_kernel_spmd

F32 = mybir.dt.float32
BF16 = mybir.dt.bfloat16
AF = mybir.ActivationFunctionType
ALU = mybir.AluOpType

T = 4096
D = 1024
KC = 8
NCORES = 8
EPS = 1e-6
DFF = 2816
NFF = 22
TT_CONV = [456] * 8 + [448]
TWO_PI = 2.0 * math.pi


class Res:
    __slots__ = ("name", "w", "r")

    def __init__(self, name):
        self.name = name
        self.w = {}
        self.r = {}


def _merge(dst, src):
    for k, v in src.items():
        if dst.get(k, 0) < v:
            dst[k] = v


class Prog:
    NDS = 12

    def __init__(self, nc):
        self.nc = nc
        self.eng = {"pe": nc.tensor, "act": nc.scalar, "dve": nc.vector, "pool": nc.gpsimd, "sp": nc.sync}
        self.csem = {e: nc.alloc_semaphore("c_" + e) for e in ("pe", "act", "dve", "pool")}
        self.ccnt = {e: 0 for e in self.csem}
        self.dsem = {q: [nc.alloc_semaphore("d_%s%d" % (q, i)) for i in range(self.NDS)] for q in ("sp", "gq")}
        self.dcnt = {q: [0] * self.NDS for q in self.dsem}
        self.dnext = {q: 0 for q in self.dsem}
        self.seen = {s: {} for s in self.eng}
        self.pending = {e: False for e in self.csem}
        self.nres = 0

    def res(self, name=None):
        self.nres += 1
        return Res(name or ("r%d" % self.nres))

    def _handle(self, key):
        if key[0] == "c":
            return self.csem[key[1]]
        return self.dsem[key[1]][key[2]]

    def _wait(self, stream, deps):
        seen = self.seen[stream]
        for key, val in deps.items():
            if val <= 0 or seen.get(key, 0) >= val:
                continue
            if stream == "pe" and key == ("c", "pe"):
                continue
            self.eng[stream].wait_ge(self._handle(key), val)
            seen[key] = val

    def _deps(self, reads, writes):
        deps = {}
        for r in reads:
            _merge(deps, r.w)
        for w in writes:
            _merge(deps, w.w)
            _merge(deps, w.r)
        return deps

    def op(self, e, fn, reads=(), writes=(), inc=True):
        self._wait(e, self._deps(reads, writes))
        ins = fn(self.eng[e])
        key = ("c", e)
        val = self.ccnt[e] + 1
        if inc:
            ins.then_inc(self.csem[e], 1)
            self.ccnt[e] = val
            self.pending[e] = False
        else:
            self.pending[e] = True
        for r in reads:
            if r.r.get(key, 0) < val:
                r.r[key] = val
        for w in writes:
            if w.w.get(key, 0) < val:
                w.w[key] = val
        return ins

    def dma(self, q, out, in_, reads=(), writes=(), **kw):
        stream = "sp" if q == "sp" else "pool"
        i = self.dnext[q]
        self.dnext[q] = (i + 1) % self.NDS
        key = ("d", q, i)
        deps = self._deps(reads, writes)
        prev = self.dcnt[q][i] * 16
        if prev and deps.get(key, 0) < prev:
            deps[key] = prev
        self._wait(stream, deps)
        ins = self.eng[stream].dma_start(out=out, in_=in_, **kw)
        self.dcnt[q][i] += 1
        val = self.dcnt[q][i] * 16
        ins.then_inc(self.dsem[q][i], 16)
        for r in reads:
            if r.r.get(key, 0) < val:
                r.r[key] = val
        for w in writes:
            if w.w.get(key, 0) < val:
                w.w[key] = val
        return ins

    def barrier(self):
        for e in self.pending:
            assert not self.pending[e], "pending un-inc'd op on " + e
        allv = {("c", e): v for e, v in self.ccnt.items()}
        for q in self.dsem:
            for i in range(self.NDS):
                allv[("d", q, i)] = self.dcnt[q][i] * 16
        for s in self.eng:
            self._wait(s, allv)


def _bf(a):
    return np.ascontiguousarray(a.astype(np.float32)).astype(ml_dtypes.bfloat16)


def _zpad(G):
    out = np.zeros((128,) + G.shape[1:], G.dtype)
    out[0:64, 0::2] = G[:, 0::2]
    out[64:128, 1::2] = G[:, 1::2]
    return out


def make_consts():
    N = 8192
    t1 = np.arange(64)[:, None, None].astype(np.float64)
    t2 = np.arange(64)[None, :, None].astype(np.float64)
    f1 = np.arange(128)[None, None, :].astype(np.float64)
    phi = 2 * np.pi * (f1 + 0.5) * (64 * t1 + t2) / N
    Gr = np.cos(phi)
    Gi = -np.sin(phi)
    tt = np.arange(64)[:, None].astype(np.float64)
    f2 = np.arange(32)[None, :].astype(np.float64)
    Cm = np.cos(2 * np.pi * tt * f2 / 64)
    Sm = np.sin(2 * np.pi * tt * f2 / 64)

    def blk(rr, ri, ir, ii):
        return np.block([[rr, ri], [ir, ii]])

    F2 = np.concatenate([blk(Cm, -Sm, Sm, Cm), blk(Sm, Cm, -Cm, Sm)], 1)
    Kf = np.concatenate([blk(Cm, Cm, Sm, Sm), blk(-Sm, -Sm, Cm, Cm)], 1)
    Kb = np.concatenate([blk(Cm, Cm, Sm, Sm), blk(Sm, Sm, -Cm, -Cm)], 1)
    CT = Cm.T
    ST = Sm.T
    Ah = np.block([[CT, ST], [-ST, CT]])
    A = np.concatenate([Ah, Ah], 0)
    RBr = (2.0 / N) * np.transpose(Gr, (2, 1, 0))
    RBi = (2.0 / N) * np.transpose(Gi, (2, 1, 0))
    c = {
        "c_gr": _bf(_zpad(Gr).reshape(128, 64 * 128)),
        "c_gi": _bf(_zpad(Gi).reshape(128, 64 * 128)),
        "c_f2": _bf(F2), "c_kf": _bf(Kf), "c_kb": _bf(Kb), "c_a": _bf(A),
        "c_rbr": _bf(RBr.reshape(128, 64 * 64)),
        "c_rbi": _bf(RBi.reshape(128, 64 * 64)),
        "c_ident": _bf(np.eye(128)),
    }
    L = T
    t01 = np.linspace(0.0, 1.0, L, dtype=np.float32)[:, None]
    w_pos = (np.float32(2.0 * math.pi) * np.arange(L, dtype=np.float32) / np.float32(L)).astype(np.float32)
    f = np.linspace(1e-4, 15, 16, dtype=np.float32)
    ang = (w_pos[:, None] * f[None, :]).astype(np.float32)
    z = np.concatenate([t01, np.cos(ang), -np.sin(ang)], axis=-1).astype(np.float32)
    c["c_zT"] = np.ascontiguousarray(z.T)
    deltas = np.abs(np.linspace(math.log(1e-2) / 0.3, math.log(1e-2) / 1.5, 512, dtype=np.float32))
    decay = np.exp(-t01 * deltas[None, :]).astype(np.float32)
    c["c_decay"] = np.ascontiguousarray(np.transpose(decay.reshape(64, 64, 512), (1, 0, 2)).reshape(32, 128, 512))
    return c


def make_rpb_table(rpb):
    q = np.arange(64)
    ws = np.clip(q - 8, 0, 48)
    kc = np.arange(64)
    ok = (kc[None, :] >= ws[:, None]) & (kc[None, :] < ws[:, None] + 16)
    dc = np.clip(kc[None, :] - q[:, None] + 15, 0, 30)
    g = rpb[:, :, dc]
    g = np.where(ok[None, None], g, np.float32(-1e30)).astype(np.float32)
    out = np.empty((8, 2, 64, 14, 64), np.float32)
    for a in range(2):
        out[:, a] = np.transpose(g[:, a:a + 14], (0, 3, 1, 2))
    return np.ascontiguousarray(out.reshape(8, 128, 14 * 64))


def col(v, nchunk):
    v = np.asarray(v, np.float32)
    if v.ndim == 1:
        return np.ascontiguousarray(v.reshape(nchunk, 128).T)
    r = v.shape[0]
    return np.ascontiguousarray(np.transpose(v.reshape(r, nchunk, 128), (2, 1, 0)))


INPUT_SPECS = {}


def build(debug=False, stop=None, skip_steps=None):
    nc = bass.Bass("TRN2", target_bir_lowering=False)
    pg = Prog(nc)
    dt_np = {F32: np.float32, BF16: ml_dtypes.bfloat16}
    ins = {}

    def inp(name, shape, dt=F32):
        INPUT_SPECS[name] = (tuple(shape), dt_np[dt])
        ins[name] = nc.dram_tensor(name, list(shape), dt, kind="ExternalInput").ap()
        return ins[name]

    dbg_names = []

    def scratch(name, shape, dt):
        kind = "ExternalOutput" if debug else "Internal"
        if debug:
            dbg_names.append(name)
        return nc.dram_tensor(name, list(shape), dt, kind=kind).ap()

    x = inp("x", [T, D])
    ccol = inp("ccol", [128, 2, 8])
    ctx = inp("ctx", [256, D])
    w_mod = inp("w_mod", [2, D, 6 * D])
    b_mod = inp("b_mod", [2, 6 * D])
    gvec = inp("gvec", [2, 4, D])
    w_in = inp("w_in", [D, 3072])
    w_out = inp("w_out", [D, D])
    hsw_col = inp("hsw_col", [128, 12, 3])
    hsb_col = inp("hsb_col", [128, 12])
    hf_w1 = inp("hf_w1", [33, 64])
    hf_w2 = inp("hf_w2", [64, 64])
    hf_w3 = inp("hf_w3", [64, 64])
    hf_w4 = inp("hf_w4", [64, 1024])
    hf_b_col = inp("hf_b_col", [64, 3])
    hf_freq_col = inp("hf_freq_col", [64, 3])
    hyb_col = inp("hyb_col", [128, 4])
    rpb_t2 = inp("rpb_t2", [8, 128, 14 * 64])
    cf_w_pw1 = inp("cf_w_pw1", [D, 2048])
    cf_b1_col = inp("cf_b1_col", [128, 16])
    cf_wdw_col = inp("cf_wdw_col", [128, 8, 31])
    cf_bdw_col = inp("cf_bdw_col", [128, 8])
    cf_lng_col = inp("cf_lng_col", [128, 8])
    cf_lnb_col = inp("cf_lnb_col", [128, 8])
    cf_w_pw2 = inp("cf_w_pw2", [D, D])
    cf_b_pw2 = inp("cf_b_pw2", [1, D])
    ffn_w_up = inp("ffn_w_up", [2, D, 2 * DFF])
    ffn_wdw_col = inp("ffn_wdw_col", [2, 128, 44, 3])
    ffn_bdw_col = inp("ffn_bdw_col", [2, 128, 44])
    ffn_w_down = inp("ffn_w_down", [2, DFF, D])
    c_gr = inp("c_gr", [128, 64 * 128], BF16)
    c_gi = inp("c_gi", [128, 64 * 128], BF16)
    c_f2 = inp("c_f2", [128, 128], BF16)
    c_kf = inp("c_kf", [128, 128], BF16)
    c_kb = inp("c_kb", [128, 128], BF16)
    c_a = inp("c_a", [128, 128], BF16)
    c_rbr = inp("c_rbr", [128, 64 * 64], BF16)
    c_rbi = inp("c_rbi", [128, 64 * 64], BF16)
    c_ident = inp("c_ident", [128, 128], BF16)
    c_zT = inp("c_zT", [33, T])
    c_decay = inp("c_decay", [32, 128, 512])

    out = nc.dram_tensor("out", [T, D], F32, kind="ExternalOutput").ap()

    hbuf = scratch("hbuf", [T, D], F32)
    modrow = scratch("modrow", [2, 8, D], F32)
    x0T = scratch("x0T", [512, T], BF16)
    vpT = scratch("vpT", [512, T], BF16)
    qT = scratch("qT", [512, T], BF16)
    kT = scratch("kT", [512, T], BF16)
    v_d = scratch("v_d", [T, 512], BF16)
    kcT = scratch("kcT", [512, 256], BF16)
    vc_d = scratch("vc_d", [256, 512], BF16)
    BdK = scratch("BdK", [128, 128, 1024], BF16)
    KKd = scratch("KKd", [128, 128, 512], BF16)
    Bd = scratch("Bd", [128, 128, 512], BF16)
    Zd = scratch("Zd", [128, 128, 512], BF16)
    yT = scratch("yT", [D, T], BF16)
    actT = scratch("actT", [DFF, T], BF16)
    u2T = scratch("u2T", [D, T], F32)

    Rd = {n: pg.res("d_" + n) for n in
          ["hbuf", "modrow", "x0T", "vpT", "qT", "kT", "v_d", "kcT", "vc_d", "BdK", "KKd", "Bd", "Zd",
           "yT", "actT", "u2T", "out"]}

    es = contextlib.ExitStack()
    with es:
        uniq = [0]

        def sb(name, shape, dt, stack=es):
            uniq[0] += 1
            return stack.enter_context(nc.sbuf_tensor("%s_%d" % (name, uniq[0]), list(shape), dt))

        psall = es.enter_context(nc.psum_tensor("psall", [128, 8 * 512], F32))
        ps = [psall[:, i * 512:(i + 1) * 512] for i in range(8)]
        Rps = [pg.res("ps%d" % i) for i in range(8)]
        H = {}
        R_hnT = pg.res("hnT")
        ident = sb("ident", [128, 128], BF16)
        R_const = pg.res("const")
        epsc = sb("epsc", [128, 1], F32)
        pg.dma("sp", ident[:], c_ident, writes=[R_const])
        pg.op("dve", lambda e: e.memset(epsc[:], EPS), writes=[R_const])

        def hn_open(stack):
            hnT = sb("hnT", [128, KC, T + 2], BF16, stack)
            H["hnT"] = hnT
            pg.op("dve", lambda e: e.memset(hnT[:, :, 0:1], 0.0), writes=[R_hnT])
            pg.op("dve", lambda e: e.memset(hnT[:, :, T + 1:T + 2], 0.0), writes=[R_hnT])

        state = {"psi": 0}

        def next_ps():
            i = state["psi"]
            state["psi"] = (i + 1) % 8
            return ps[i], Rps[i]

        def skew(n_iter, stage_fns, reverse=True):
            S = len(stage_fns)
            for it in range(n_iter + S - 1):
                for si in (range(S - 1, -1, -1) if reverse else range(S)):
                    i = it - si
                    if 0 <= i < n_iter:
                        stage_fns[si](i)

        def rstd_from_ss(ss_ap, lnv_ap, rstd_ap, R):
            pg.op("act", lambda e: e.activation(out=lnv_ap, in_=ss_ap, func=AF.Ln, bias=epsc[0:ss_ap.shape[0], :],
                                                scale=1.0 / D), reads=[R, R_const], writes=[R])
            pg.op("act", lambda e: e.activation(out=rstd_ap, in_=lnv_ap, func=AF.Exp, scale=-0.5),
                  reads=[R], writes=[R])

        def phase_mod(l):
            with contextlib.ExitStack() as st:
                cc = sb("m_cc", [128, 2, 8], F32, st)
                lhs = sb("m_lhs", [128, KC, 33], BF16, st)
                mrow = sb("m_row", [33, 6 * D], F32, st)
                brow = sb("m_brow", [33, 6 * D], F32, st)
                grow = sb("m_grow", [33, 4, D], F32, st)
                orow = sb("m_orow", [33, 6, D], F32, st)
                wt = [sb("m_w%d" % i, [128, KC, 512], BF16, st) for i in range(2)]
                Rw = [pg.res() for _ in range(2)]
                R = pg.res("modtmp")
                pg.dma("sp", cc[:], ccol, writes=[R])
                pg.dma("sp", brow[0:1, :], b_mod[l:l + 1, :], writes=[R])
                pg.dma("sp", brow[32:33, :], b_mod[l:l + 1, :], writes=[R])
                pg.dma("sp", grow[0:1, :, :], gvec[l:l + 1, :, :], writes=[R])
                pg.dma("sp", grow[32:33, :, :], gvec[l:l + 1, :, :], writes=[R])
                pg.op("dve", lambda e: e.memset(lhs[:], 0.0), writes=[R])
                pg.op("act", lambda e: e.activation(out=lhs[:, :, 0], in_=cc[:, 0, :], func=AF.Silu),
                      reads=[R], writes=[R])
                pg.op("act", lambda e: e.activation(out=lhs[:, :, 32], in_=cc[:, 1, :], func=AF.Silu),
                      reads=[R], writes=[R])
                wsrc = w_mod[l].rearrange("(kc p) n -> p kc n", p=128)
                for cg in range(12):
                    b = cg % 2
                    pg.dma("gq", wt[b][:], wsrc[:, :, cg * 512:(cg + 1) * 512], writes=[Rw[b]])
                    pt, Rp = next_ps()
                    for kc in range(KC):
                        pg.op("pe", lambda e: e.matmul(pt[0:33, :], lhs[:, kc, :], wt[b][:, kc, :],
                                                       start=(kc == 0), stop=(kc == KC - 1)),
                              reads=[R, Rw[b]], writes=[Rp], inc=(kc == KC - 1))
                    pg.op("act", lambda e: e.activation(out=mrow[:, cg * 512:(cg + 1) * 512], in_=pt[0:33, :],
                                                        func=AF.Identity), reads=[Rp], writes=[R])
                for p0 in (0, 32):
                    pg.op("dve", lambda e: e.tensor_tensor(out=mrow[p0:p0 + 1, :], in0=mrow[p0:p0 + 1, :],
                                                           in1=brow[p0:p0 + 1, :], op=ALU.add),
                          reads=[R], writes=[R])

                def seg(p0, i):
                    return mrow[p0:p0 + 1, i * D:(i + 1) * D]
                for (dst, sc_i, g_i) in ((0, 1, 0), (3, 4, 2)):
                    pg.op("dve", lambda e: e.scalar_tensor_tensor(out=orow[0:1, dst, :], in0=seg(0, sc_i), scalar=1.0,
                                                                  in1=grow[0:1, g_i, :], op0=ALU.add, op1=ALU.mult),
                          reads=[R], writes=[R])
                for (dst, sh_i) in ((1, 0), (4, 3)):
                    pg.op("dve", lambda e: e.tensor_copy(out=orow[0:1, dst, :], in_=seg(0, sh_i)),
                          reads=[R], writes=[R])
                for (dst, gt_i, g_i) in ((2, 2, 1), (5, 5, 3)):
                    pg.op("dve", lambda e: e.tensor_tensor(out=orow[0:1, dst, :], in0=seg(0, gt_i),
                                                           in1=grow[0:1, g_i, :], op=ALU.mult),
                          reads=[R], writes=[R])
                pg.op("dve", lambda e: e.scalar_tensor_tensor(out=orow[32:33, 0, :], in0=seg(32, 1), scalar=1.0,
                                                              in1=grow[32:33, 0, :], op0=ALU.add, op1=ALU.mult),
                      reads=[R], writes=[R])
                pg.op("dve", lambda e: e.tensor_copy(out=orow[32:33, 1, :], in_=seg(32, 0)), reads=[R], writes=[R])
                pg.dma("sp", modrow[l:l + 1, 0:6, :], orow[0:1, :, :], reads=[R], writes=[Rd["modrow"]])
                pg.dma("sp", modrow[l:l + 1, 6:8, :], orow[32:33, 0:2, :], reads=[R], writes=[Rd["modrow"]])
                pg.barrier()

        def load_bc(dst, l, row, R):
            pg.dma("sp", dst[:], modrow[l, row:row + 1, :].partition_broadcast(128)[:, 0, :],
                   reads=[Rd["modrow"]], writes=[R])

        def phase_prenorm(src, Rsrc, ntok, l, row_gs, row_sh, dstT, R_dst, off, ext_stack=None):
            with contextlib.ExitStack() as st_own:
                st = ext_stack if ext_stack is not None else st_own
                gs = sb("pn_gs", [128, D], F32, st)
                sh = sb("pn_sh", [128, D], BF16, st)
                Rg = pg.res()
                load_bc(gs, l, row_gs, Rg)
                pg.dma("gq", sh[:], modrow[l, row_sh:row_sh + 1, :].partition_broadcast(128)[:, 0, :],
                       reads=[Rd["modrow"]], writes=[Rg])
                NB = 2
                NH = 4
                hin = [sb("pn_h%d" % i, [128, D], F32, st) for i in range(NH)]
                t1 = [sb("pn_t1%d" % i, [128, D], F32, st) for i in range(NB)]
                t2 = [sb("pn_t2%d" % i, [128, D], BF16, st) for i in range(NB)]
                hn = [sb("pn_hn%d" % i, [128, D], BF16, st) for i in range(NB)]
                junk = sb("pn_junk", [128, D], BF16, st)
                small = [sb("pn_s%d" % i, [128, 4], F32, st) for i in range(NB)]
                Rh = [pg.res() for _ in range(NH)]
                Rt1 = [pg.res() for _ in range(NB)]
                Rt2 = [pg.res() for _ in range(NB)]
                Rhn = [pg.res() for _ in range(NB)]
                Rs = [pg.res() for _ in range(NB)]
                Rj = pg.res()
                ntile = ntok // 128
                for i0_ in range(min(2, ntile)):
                    pg.dma("sp", hin[i0_ % NH][:], src[i0_ * 128:(i0_ + 1) * 128, :], reads=[Rsrc],
                           writes=[Rh[i0_ % NH]])

                def s0(i):
                    i2 = i + 2
                    if i2 < ntile:
                        pg.dma("sp", hin[i2 % NH][:], src[i2 * 128:(i2 + 1) * 128, :], reads=[Rsrc],
                               writes=[Rh[i2 % NH]])

                def s1(i):
                    b = i % NB
                    hb_ = i % NH
                    pg.op("act", lambda e: e.activation(out=junk[:], in_=hin[hb_][:], func=AF.Square,
                                                        accum_out=small[b][:, 0:1]),
                          reads=[Rh[hb_]], writes=[Rs[b]])
                    rstd_from_ss(small[b][:, 0:1], small[b][:, 1:2], small[b][:, 2:3], Rs[b])
                    pg.op("act", lambda e: e.activation(out=t1[b][:], in_=hin[hb_][:], func=AF.Copy,
                                                        scale=small[b][:, 2:3]),
                          reads=[Rh[hb_], Rs[b]], writes=[Rt1[b]])

                def s2(i):
                    b = i % NB
                    pg.op("pool", lambda e: e.tensor_tensor(out=t2[b][:], in0=t1[b][:], in1=gs[:], op=ALU.mult),
                          reads=[Rt1[b], Rg], writes=[Rt2[b]])

                def s3(i):
                    b = i % NB
                    pg.op("dve", lambda e: e.tensor_tensor(out=hn[b][:], in0=t2[b][:], in1=sh[:], op=ALU.add),
                          reads=[Rt2[b], Rg], writes=[Rhn[b]])

                def s4(i):
                    b = i % NB
                    ptb = ps[i % 2].bitcast(BF16)
                    for kc in range(KC):
                        pg.op("pe", lambda e: e.transpose(ptb[:, kc * 128:(kc + 1) * 128],
                                                          hn[b][:, kc * 128:(kc + 1) * 128], ident[:]),
                              reads=[Rhn[b], R_const], writes=[Rps[i % 2]], inc=(kc == KC - 1))

                def s5(i):
                    ptb = ps[i % 2].bitcast(BF16)
                    pg.op("dve", lambda e: e.tensor_copy(
                        out=dstT[:, :, off + i * 128: off + (i + 1) * 128],
                        in_=ptb[:].rearrange("p (k t) -> p k t", k=KC)), reads=[Rps[i % 2]], writes=[R_dst])

                skew(ntok // 128, [s0, s1, s2, s3, s4, s5])
                if ext_stack is None:
                    pg.barrier()

        class RU:
            NB = 3

            def __init__(self, st, l, row_gg, tag):
                NB = self.NB
                self.gg = sb(tag + "_gg", [128, D], F32, st)
                self.Rg = pg.res()
                load_bc(self.gg, l, row_gg, self.Rg)
                self.hin = [sb(tag + "_h%d" % i, [128, D], F32, st) for i in range(NB)]
                self.tmp = [sb(tag + "_t%d" % i, [128, D], F32, st) for i in range(NB)]
                self.hout = [sb(tag + "_o%d" % i, [128, D], F32, st) for i in range(NB)]
                self.small = [sb(tag + "_s%d" % i, [128, 8], F32, st) for i in range(NB)]
                self.junk = sb(tag + "_j", [128, 512], BF16, st)
                self.Rh = [pg.res() for _ in range(NB)]
                self.Rt = [pg.res() for _ in range(NB)]
                self.Ro = [pg.res() for _ in range(NB)]
                self.Rs = [pg.res() for _ in range(NB)]
                self.Rj = pg.res()

            def load(self, i, hsrc, Rsrc):
                b = i % self.NB
                pg.dma("sp", self.hin[b][:], hsrc[i * 128:(i + 1) * 128, :], reads=[Rsrc], writes=[self.Rh[b]])

            def scale(self, i, yh, Ry):
                b = i % self.NB
                sm = self.small[b]
                for hf in range(2):
                    pg.op("act", lambda e: e.activation(out=self.junk[:], in_=yh[hf], func=AF.Square,
                                                        accum_out=sm[:, hf:hf + 1]),
                          reads=[Ry[hf]], writes=[self.Rs[b]])
                pg.op("dve", lambda e: e.tensor_tensor(out=sm[:, 2:3], in0=sm[:, 0:1], in1=sm[:, 1:2], op=ALU.add),
                      reads=[self.Rs[b]], writes=[self.Rs[b]])
                rstd_from_ss(sm[:, 2:3], sm[:, 3:4], sm[:, 4:5], self.Rs[b])
                for hf in range(2):
                    cs = slice(hf * 512, (hf + 1) * 512)
                    pg.op("dve", lambda e: e.scalar_tensor_tensor(out=self.tmp[b][:, cs], in0=yh[hf],
                                                                  scalar=sm[:, 4:5], in1=self.gg[:, cs],
                                                                  op0=ALU.mult, op1=ALU.mult),
                          reads=[Ry[hf], self.Rs[b], self.Rg], writes=[self.Rt[b]])

            def finish(self, i, hdst, Rdst):
                b = i % self.NB
                pg.op("pool", lambda e: e.tensor_tensor(out=self.hout[b][:], in0=self.hin[b][:], in1=self.tmp[b][:],
                                                        op=ALU.add),
                      reads=[self.Rh[b], self.Rt[b]], writes=[self.Ro[b]])
                pg.dma("sp", hdst[i * 128:(i + 1) * 128, :], self.hout[b][:], reads=[self.Ro[b]], writes=[Rdst])

        def conv3_evac(pt, Rp, n, wcol, bcol, Rw, t0, t1, dst, Rt, Rdst):
            pg.op("act", lambda e: e.activation(out=t0[:, 0:n], in_=pt[:, 1:n + 1], func=AF.Identity,
                                                bias=bcol, scale=wcol[:, 1:2]),
                  reads=[Rp, Rw], writes=[Rt])
            pg.op("dve", lambda e: e.scalar_tensor_tensor(out=t1[:, 0:n], in0=pt[:, 0:n], scalar=wcol[:, 0:1],
                                                          in1=t0[:, 0:n], op0=ALU.mult, op1=ALU.add),
                  reads=[Rp, Rw, Rt], writes=[Rt])
            pg.op("dve", lambda e: e.scalar_tensor_tensor(out=dst, in0=pt[:, 2:n + 2], scalar=wcol[:, 2:3],
                                                          in1=t1[:, 0:n], op0=ALU.mult, op1=ALU.add),
                  reads=[Rp, Rw, Rt], writes=[Rdst])

        def load_w(dst, W, c0, n, R):
            pg.dma("gq", dst[:, :, 0:n], W.rearrange("(kc p) n -> p kc n", p=128)[:, :, c0:c0 + n], writes=[R])

        def phase_proj0():
            hnT = H["hnT"]
            with contextlib.ExitStack() as st:
                hsw = sb("p0_hsw", [128, 12, 3], F32, st)
                hsb = sb("p0_hsb", [128, 12], F32, st)
                Rc = pg.res()
                pg.dma("sp", hsw[:], hsw_col, writes=[Rc])
                pg.dma("sp", hsb[:], hsb_col, writes=[Rc])
                x1s = sb("p0_x1s", [128, 4, T], BF16, st)
                R_x1 = pg.res()
                wt = [sb("p0_w%d" % i, [128, KC, 128], BF16, st) for i in range(3)]
                Rw = [pg.res() for _ in range(3)]
                stg = [sb("p0_stg%d" % i, [128, T], BF16, st) for i in range(2)]
                Rstg = [pg.res() for _ in range(2)]
                t0 = [sb("p0_t0%d" % i, [128, 512], F32, st) for i in range(2)]
                t1 = [sb("p0_t1%d" % i, [128, 512], F32, st) for i in range(2)]
                t2 = [sb("p0_t2%d" % i, [128, 512], F32, st) for i in range(2)]
                Rt = [pg.res() for _ in range(2)]
                Rt2 = [pg.res() for _ in range(2)]
                order = [4, 5, 6, 7, 8, 9, 10, 11, 0, 1, 2, 3]
                nblk = 0
                load_w(wt[0], w_in, order[0] * 128, 128, Rw[0])
                for ci, c in enumerate(order):
                    b = ci % 2
                    wb = ci % 3
                    if ci + 1 < len(order):
                        load_w(wt[(ci + 1) % 3], w_in, order[ci + 1] * 128, 128, Rw[(ci + 1) % 3])
                    s = 0
                    for n in TT_CONV:
                        tb = nblk % 2
                        nblk += 1
                        pt, Rp = next_ps()
                        for kc in range(KC):
                            pg.op("pe", lambda e: e.matmul(pt[:, 0:n + 2], wt[wb][:, kc, :], hnT[:, kc, s:s + n + 2],
                                                           start=(kc == 0), stop=(kc == KC - 1)),
                                  reads=[Rw[wb], R_hnT], writes=[Rp], inc=(kc == KC - 1))
                        if 4 <= c < 8:
                            conv3_evac(pt, Rp, n, hsw[:, c, :], hsb[:, c:c + 1], Rc, t0[tb], t1[tb],
                                       x1s[:, c - 4, s:s + n], Rt[tb], R_x1)
                        elif c >= 8:
                            conv3_evac(pt, Rp, n, hsw[:, c, :], hsb[:, c:c + 1], Rc, t0[tb], t1[tb],
                                       t2[tb][:, 0:n], Rt[tb], Rt2[tb])
                            pg.op("dve", lambda e: e.tensor_tensor(out=stg[b][:, s:s + n], in0=t2[tb][:, 0:n],
                                                                   in1=x1s[:, c - 8, s:s + n], op=ALU.mult),
                                  reads=[Rt2[tb], R_x1], writes=[Rstg[b]])
                        else:
                            conv3_evac(pt, Rp, n, hsw[:, c, :], hsb[:, c:c + 1], Rc, t0[tb], t1[tb],
                                       stg[b][:, s:s + n], Rt[tb], Rstg[b])
                        s += n
                    if c >= 8:
                        pg.dma("sp", vpT[(c - 8) * 128:(c - 7) * 128, :], stg[b][:], reads=[Rstg[b]],
                               writes=[Rd["vpT"]])
                    elif c < 4:
                        pg.dma("sp", x0T[c * 128:(c + 1) * 128, :], stg[b][:], reads=[Rstg[b]], writes=[Rd["x0T"]])
                pg.barrier()
            with contextlib.ExitStack() as st:
                wt = [sb("p1_w%d" % i, [128, KC, 128], BF16, st) for i in range(2)]
                Rw = [pg.res() for _ in range(2)]
                stg = [sb("p1_stg%d" % i, [128, T], BF16, st) for i in range(2)]
                Rstg = [pg.res() for _ in range(2)]
                wk = sb("p1_wk", [128, KC, 512], BF16, st)
                wv = sb("p1_wv", [128, KC, 512], BF16, st)
                Rwk = pg.res()
                load_w(wk, w_in, 2048, 512, Rwk)
                load_w(wv, w_in, 2560, 512, Rwk)
                cnT = sb("p1_cnT", [128, KC, 256], BF16, st)
                R_cn = pg.res()
                phase_prenorm(ctx, pg.res(), 256, 0, 6, 7, cnT, R_cn, 0, ext_stack=st)
                for ci in range(8):
                    b = ci % 2
                    isq = ci < 4
                    c0 = 1536 + ci * 128
                    load_w(wt[b], w_in, c0, 128, Rw[b])
                    for ti in range(8):
                        pt, Rp = next_ps()
                        for kc in range(KC):
                            pg.op("pe", lambda e: e.matmul(pt[:], wt[b][:, kc, :],
                                                           hnT[:, kc, 1 + ti * 512:1 + (ti + 1) * 512],
                                                           start=(kc == 0), stop=(kc == KC - 1)),
                                  reads=[Rw[b], R_hnT], writes=[Rp], inc=(kc == KC - 1))
                        if ti % 2 == 0:
                            pg.op("act", lambda e: e.activation(out=stg[b][:, ti * 512:(ti + 1) * 512], in_=pt[:],
                                                                func=AF.Copy, scale=(0.125 if isq else 1.0)),
                                  reads=[Rp], writes=[Rstg[b]])
                        else:
                            pg.op("dve", lambda e: e.tensor_scalar(out=stg[b][:, ti * 512:(ti + 1) * 512], in0=pt[:],
                                                                   scalar1=(0.125 if isq else 1.0), scalar2=None,
                                                                   op0=ALU.mult),
                                  reads=[Rp], writes=[Rstg[b]])
                    dstT, nm = (qT, "qT") if isq else (kT, "kT")
                    cc = ci % 4
                    pg.dma("sp", dstT[cc * 128:(cc + 1) * 128, :], stg[b][:], reads=[Rstg[b]], writes=[Rd[nm]])
                vst = [sb("p1_vst%d" % i, [128, 512], BF16, st) for i in range(2)]
                Rvst = [pg.res() for _ in range(2)]
                for i in range(32):
                    b = i % 2
                    pt, Rp = next_ps()
                    for kc in range(KC):
                        pg.op("pe", lambda e: e.matmul(pt[:], hnT[:, kc, 1 + i * 128:1 + (i + 1) * 128], wv[:, kc, :],
                                                       start=(kc == 0), stop=(kc == KC - 1)),
                              reads=[Rwk, R_hnT], writes=[Rp], inc=(kc == KC - 1))
                    if i % 2 == 0:
                        pg.op("act", lambda e: e.activation(out=vst[b][:], in_=pt[:], func=AF.Copy),
                              reads=[Rp], writes=[Rvst[b]])
                    else:
                        pg.op("dve", lambda e: e.tensor_copy(out=vst[b][:], in_=pt[:]), reads=[Rp], writes=[Rvst[b]])
                    pg.dma("sp", v_d[i * 128:(i + 1) * 128, :], vst[b][:], reads=[Rvst[b]], writes=[Rd["v_d"]])
                kst = sb("p1_kst", [128, 4, 256], BF16, st)
                Rk = pg.res()
                for cc in range(4):
                    pt, Rp = next_ps()
                    for kc in range(KC):
                        pg.op("pe", lambda e: e.matmul(pt[:, 0:256], wk[:, kc, cc * 128:(cc + 1) * 128], cnT[:, kc, :],
                                                       start=(kc == 0), stop=(kc == KC - 1)),
                              reads=[Rwk, R_cn], writes=[Rp], inc=(kc == KC - 1))
                    pg.op("act", lambda e: e.activation(out=kst[:, cc, :], in_=pt[:, 0:256], func=AF.Copy),
                          reads=[Rp], writes=[Rk])
                pg.dma("sp", kcT.rearrange("(c p) t -> p c t", p=128), kst[:], reads=[Rk], writes=[Rd["kcT"]])
                for i in range(2):
                    pt, Rp = next_ps()
                    for kc in range(KC):
                        pg.op("pe", lambda e: e.matmul(pt[:], cnT[:, kc, i * 128:(i + 1) * 128], wv[:, kc, :],
                                                       start=(kc == 0), stop=(kc == KC - 1)),
                              reads=[Rwk, R_cn], writes=[Rp], inc=(kc == KC - 1))
                    pg.op("act", lambda e: e.activation(out=vst[i][:], in_=pt[:], func=AF.Copy),
                          reads=[Rp], writes=[Rvst[i]])
                    pg.dma("sp", vc_d[i * 128:(i + 1) * 128, :], vst[i][:], reads=[Rvst[i]], writes=[Rd["vc_d"]])
                pg.barrier()

        def phase_filters():
          with contextlib.ExitStack() as st0:
            h3 = sb("f_h3", [64, T], BF16, st0)
            R3 = pg.res()
            with contextlib.ExitStack() as st:
                zT = sb("f_zT", [33, T], F32, st)
                w1 = sb("f_w1", [33, 64], F32, st)
                w2 = sb("f_w2", [64, 64], F32, st)
                w3 = sb("f_w3", [64, 64], F32, st)
                bc = sb("f_b", [64, 3], F32, st)
                fq = sb("f_fq", [64, 3], F32, st)
                fs = sb("f_fs", [64, 3], F32, st)
                fb = sb("f_fb", [64, 3], F32, st)
                hA = sb("f_hA", [64, T], F32, st)
                hB = sb("f_hB", [64, T], F32, st)
                sa = [sb("f_sa%d" % i, [64, 512], F32, st) for i in range(2)]
                sb_ = [sb("f_sb%d" % i, [64, 512], F32, st) for i in range(2)]
                Rs = [pg.res() for _ in range(2)]
                R = pg.res()
                RA, RB = pg.res(), pg.res()
                pg.dma("sp", zT[:], c_zT, writes=[R])
                pg.dma("sp", w1[:], hf_w1, writes=[R])
                pg.dma("sp", w2[:], hf_w2, writes=[R])
                pg.dma("sp", w3[:], hf_w3, writes=[R])
                pg.dma("sp", bc[:], hf_b_col, writes=[R])
                pg.dma("sp", fq[:], hf_freq_col, writes=[R])
                pg.op("dve", lambda e: e.tensor_scalar(out=fs[:], in0=fq[:], scalar1=1.0 / TWO_PI, scalar2=None,
                                                       op0=ALU.mult), reads=[R], writes=[R])
                pg.op("dve", lambda e: e.tensor_tensor(out=fb[:], in0=fs[:], in1=bc[:], op=ALU.mult),
                      reads=[R], writes=[R])
                layers = [(w1, 33, zT, R, hA, RA), (w2, 64, hA, RA, hB, RB), (w3, 64, hB, RB, h3, R3)]
                nb = 0
                for li, (w, K, src, Rsrc, dst, Rdst) in enumerate(layers):
                    for ti in range(8):
                        b = nb % 2
                        nb += 1
                        cs = slice(ti * 512, (ti + 1) * 512)
                        pt, Rp = next_ps()
                        pg.op("pe", lambda e: e.matmul(pt[0:64, :], w[0:K, :], src[0:K, cs], start=True, stop=True),
                              reads=[R, Rsrc], writes=[Rp])
                        pg.op("dve", lambda e: e.tensor_scalar(out=sa[b][:], in0=pt[0:64, :], scalar1=fs[:, li:li + 1],
                                                               scalar2=fb[:, li:li + 1], op0=ALU.mult, op1=ALU.add),
                              reads=[Rp, R], writes=[Rs[b]])
                        pg.op("dve", lambda e: e.scalar_tensor_tensor(out=sb_[b][:], in0=sa[b][:], scalar=0.5,
                                                                      in1=sa[b][:], op0=ALU.is_gt, op1=ALU.subtract),
                              reads=[Rs[b]], writes=[Rs[b]])
                        pg.op("dve", lambda e: e.scalar_tensor_tensor(out=sb_[b][:], in0=sa[b][:], scalar=-0.5,
                                                                      in1=sb_[b][:], op0=ALU.is_lt, op1=ALU.subtract),
                              reads=[Rs[b]], writes=[Rs[b]])
                        pg.op("act", lambda e: e.activation(out=dst[:, cs], in_=sb_[b][:], func=AF.Sin,
                                                            scale=TWO_PI * (1.0 - 1e-6)),
                              reads=[Rs[b]], writes=[Rdst])
                pg.barrier()
            with contextlib.ExitStack() as st:
                R = pg.res()
                w4 = sb("f_w4", [128, 1024], BF16, st)
                h3p = sb("f_h3p", [128, T], BF16, st)
                pg.op("pool", lambda e: e.memset(h3p[64:128, :], 0.0), writes=[R3])
                pg.op("dve", lambda e: e.tensor_copy(out=h3p[0:64, :].rearrange("p (b a) -> p b a", a=64),
                                                     in_=h3[:].rearrange("p (a b) -> p b a", b=64)),
                      reads=[R3], writes=[R3])
                gr = sb("f_gr", [128, 64 * 128], BF16, st)
                gi = sb("f_gi", [128, 64 * 128], BF16, st)
                pg.op("dve", lambda e: e.memset(w4[64:128, :], 0.0), writes=[R])
                pg.dma("gq", w4[0:64, :], hf_w4, writes=[R])
                pg.dma("sp", gr[:], c_gr, writes=[R])
                pg.dma("sp", gi[:], c_gi, writes=[R])
                dec = [sb("f_dec%d" % i, [128, 4, 512], F32, st) for i in range(2)]
                Rdec = [pg.res() for _ in range(2)]
                Hs = [sb("f_Hs%d" % i, [128, 1024], BF16, st) for i in range(2)]
                RHs = [pg.res() for _ in range(2)]
                stg = [sb("f_stg%d" % i, [128, 2, 1024], BF16, st) for i in range(2)]
                Rstg = [pg.res() for _ in range(2)]
                BdKv = BdK.rearrange("(part t) f c -> t f part c", part=2)

                def load_dec(g):
                    pg.dma("sp", dec[g % 2][:], c_decay[g * 4:(g + 1) * 4].rearrange("pr p c -> p pr c"),
                           writes=[Rdec[g % 2]])

                load_dec(0)

                def s0(t2i):
                    g = t2i // 8
                    if t2i % 8 == 4 and g + 1 < 8:
                        load_dec(g + 1)
                    if t2i % 2:
                        return
                    pr = t2i // 2
                    for hb in range(2):
                        bk = 2 * (pr % 2) + hb
                        pg.op("pe", lambda e: e.matmul(ps[bk][:], h3p[:, pr * 128:(pr + 1) * 128],
                                                       w4[:, hb * 512:(hb + 1) * 512],
                                                       start=True, stop=True), reads=[R3, R], writes=[Rps[bk]])

                def s1(t2i):
                    if t2i % 2:
                        return
                    g = t2i // 8
                    pr = t2i // 2
                    b = pr % 2
                    for hb in range(2):
                        bk = 2 * (pr % 2) + hb
                        pg.op("dve", lambda e: e.tensor_tensor(out=Hs[b][:, hb * 512:(hb + 1) * 512],
                                                               in0=ps[bk][:], in1=dec[g % 2][:, pr % 4, :],
                                                               op=ALU.mult),
                              reads=[Rps[bk], Rdec[g % 2]], writes=[RHs[b]])
                    if t2i == 0:
                        pg.op("dve", lambda e: e.memset(Hs[b][0:1, 512:1024], 0.0), writes=[RHs[b]])

                def s2(t2i):
                    b = (t2i // 2) % 2
                    for part, gm in enumerate((gr, gi)):
                        for hb in range(2):
                            bk = 4 + 2 * part + hb
                            pg.op("pe", lambda e: e.matmul(ps[bk][:], gm[:, t2i * 128:(t2i + 1) * 128],
                                                           Hs[b][:, hb * 512:(hb + 1) * 512], start=True, stop=True),
                                  reads=[R, RHs[b]], writes=[Rps[bk]])

                def s3(t2i):
                    b = t2i % 2
                    for part in range(2):
                        for hb in range(2):
                            bk = 4 + 2 * part + hb
                            if hb == 0:
                                pg.op("act", lambda e: e.activation(out=stg[b][:, part, 0:512], in_=ps[bk][:],
                                                                    func=AF.Copy), reads=[Rps[bk]], writes=[Rstg[b]])
                            else:
                                pg.op("dve", lambda e: e.tensor_copy(out=stg[b][:, part, 512:1024], in_=ps[bk][:]),
                                      reads=[Rps[bk]], writes=[Rstg[b]])
                    pg.dma("sp", BdKv[t2i], stg[b][:], reads=[Rstg[b]], writes=[Rd["BdK"]])

                skew(64, [s0, s1, s2, s3])
                pg.barrier()
            with contextlib.ExitStack() as st:
                kf = sb("k_kf", [128, 128], BF16, st)
                kb = sb("k_kb", [128, 128], BF16, st)
                R = pg.res()
                pg.dma("sp", kf[:], c_kf, writes=[R])
                pg.dma("sp", kb[:], c_kb, writes=[R])
                NG = 3
                Bt = [sb("k_Bt%d" % i, [128, 8, 1024], BF16, st) for i in range(NG)]
                RBt = [pg.res() for _ in range(NG)]
                Kst = [sb("k_st%d" % i, [128, 8, 512], BF16, st) for i in range(2)]
                RKst = [pg.res() for _ in range(2)]

                def load_group(g):
                    pg.dma("sp", Bt[g % NG][:], BdK[:, g * 8:(g + 1) * 8, :], reads=[Rd["BdK"]], writes=[RBt[g % NG]])

                load_group(0)

                def s0(n):
                    g, i = n // 8, n % 8
                    if i == 0 and g + 1 < 16:
                        load_group(g + 1)
                    b = g % NG
                    pt, Rp = ps[n % 4], Rps[n % 4]
                    pg.op("pe", lambda e: e.matmul(pt[:], kf[:], Bt[b][:, i, 0:512], start=True, stop=False),
                          reads=[R, RBt[b]], writes=[Rp], inc=False)
                    pg.op("pe", lambda e: e.matmul(pt[:], kb[:], Bt[b][:, i, 512:1024], start=False, stop=True),
                          reads=[R, RBt[b]], writes=[Rp])

                def s1(n):
                    g, i = n // 8, n % 8
                    kb_ = g % 2
                    pt, Rp = ps[n % 4], Rps[n % 4]
                    if i % 2 == 0:
                        pg.op("act", lambda e: e.activation(out=Kst[kb_][:, i, :], in_=pt[:], func=AF.Copy),
                              reads=[Rp], writes=[RKst[kb_]])
                    else:
                        pg.op("dve", lambda e: e.tensor_copy(out=Kst[kb_][:, i, :], in_=pt[:]),
                              reads=[Rp], writes=[RKst[kb_]])
                    if i == 7:
                        pg.dma("sp", KKd[:, g * 8:(g + 1) * 8, :], Kst[kb_][:], reads=[RKst[kb_]], writes=[Rd["KKd"]])

                skew(128, [s0, s1])
                pg.barrier()

        def phase_hyena():
            with contextlib.ExitStack() as st:
                vp = sb("h_vp", [128, 4, T], BF16, st)
                gr = sb("h_gr", [128, 64 * 128], BF16, st)
                gi = sb("h_gi", [128, 64 * 128], BF16, st)
                R = pg.res()
                for c in range(4):
                    pg.dma("sp", vp[:, c, :], vpT[c * 128:(c + 1) * 128, :], reads=[Rd["vpT"]], writes=[R])
                pg.dma("sp", gr[:], c_gr, writes=[R])
                pg.dma("sp", gi[:], c_gi, writes=[R])
                vpp = sb("h_vpp", [128, 4, T], BF16, st)
                for c in range(4):
                    eng_ = "dve" if c % 2 == 0 else "pool"
                    pg.op(eng_, lambda e: e.tensor_copy(out=vpp[:, c, :].rearrange("p (b a) -> p b a", a=64),
                                                        in_=vp[:, c, :].rearrange("p (a b) -> p b a", b=64)),
                          reads=[R], writes=[R])
                Xs = [sb("h_Xs%d" % i, [128, 512], BF16, st) for i in range(2)]
                RXs = [pg.res() for _ in range(2)]
                stg = [sb("h_stg%d" % i, [128, 2, 512], BF16, st) for i in range(2)]
                Rstg = [pg.res() for _ in range(2)]
                Bdv = Bd.rearrange("(part t) f c -> t f part c", part=2)

                def s0(t2i):
                    if t2i % 2:
                        return
                    pr = t2i // 2
                    bk = pr % 2
                    ptb = ps[bk].bitcast(BF16)
                    for c in range(4):
                        pg.op("pe", lambda e: e.transpose(ptb[:, c * 128:(c + 1) * 128],
                                                          vpp[:, c, pr * 128:(pr + 1) * 128], ident[:]),
                              reads=[R, R_const], writes=[Rps[bk]], inc=(c == 3))

                def s1(t2i):
                    if t2i % 2:
                        return
                    pr = t2i // 2
                    bk = pr % 2
                    ptb = ps[bk].bitcast(BF16)
                    pg.op("dve", lambda e: e.tensor_copy(out=Xs[pr % 2][:], in_=ptb[:, 0:512]), reads=[Rps[bk]],
                          writes=[RXs[pr % 2]])

                def s2(t2i):
                    xb = (t2i // 2) % 2
                    for part, gm in enumerate((gr, gi)):
                        bk = 2 + 2 * (t2i % 2) + part
                        pg.op("pe", lambda e: e.matmul(ps[bk][:], gm[:, t2i * 128:(t2i + 1) * 128], Xs[xb][:],
                                                       start=True, stop=True), reads=[R, RXs[xb]],
                              writes=[Rps[bk]])

                def s3(t2i):
                    b = t2i % 2
                    for part in range(2):
                        bk = 2 + 2 * (t2i % 2) + part
                        if part == 0:
                            pg.op("act", lambda e: e.activation(out=stg[b][:, part, :], in_=ps[bk][:], func=AF.Copy),
                                  reads=[Rps[bk]], writes=[Rstg[b]])
                        else:
                            pg.op("dve", lambda e: e.tensor_copy(out=stg[b][:, part, :], in_=ps[bk][:]),
                                  reads=[Rps[bk]], writes=[Rstg[b]])
                    pg.dma("sp", Bdv[t2i], stg[b][:], reads=[Rstg[b]], writes=[Rd["Bd"]])

                skew(64, [s0, s1, s2, s3])
                pg.barrier()
            with contextlib.ExitStack() as st:
                f2 = sb("h_f2", [128, 128], BF16, st)
                am = sb("h_a", [128, 128], BF16, st)
                R = pg.res()
                pg.dma("sp", f2[:], c_f2, writes=[R])
                pg.dma("sp", am[:], c_a, writes=[R])
                NG = 3
                Bt = [sb("h_Bt%d" % i, [128, 8, 512], BF16, st) for i in range(NG)]
                KKt = [sb("h_KK%d" % i, [128, 8, 512], BF16, st) for i in range(NG)]
                RBt = [pg.res() for _ in range(NG)]
                NP = 3
                Pm = [sb("h_P%d" % i, [128, 512], BF16, st) for i in range(NP)]
                RP = [pg.res() for _ in range(NP)]
                Zst = [sb("h_Zst%d" % i, [128, 8, 512], BF16, st) for i in range(2)]
                RZst = [pg.res() for _ in range(2)]

                def load_group(g):
                    b = g % NG
                    pg.dma("sp", Bt[b][:], Bd[:, g * 8:(g + 1) * 8, :], reads=[Rd["Bd"]], writes=[RBt[b]])
                    pg.dma("sp", KKt[b][:], KKd[:, g * 8:(g + 1) * 8, :], reads=[Rd["KKd"]], writes=[RBt[b]])

                load_group(0)
                def s0(n):
                    g, i = n // 8, n % 8
                    if i == 0 and g + 1 < 16:
                        load_group(g + 1)
                    b = g % NG
                    pt, Rp = ps[n % 2], Rps[n % 2]
                    pg.op("pe", lambda e: e.matmul(pt[:], f2[:], Bt[b][:, i, :], start=True, stop=True),
                          reads=[R, RBt[b]], writes=[Rp])

                def s1(n):
                    g, i = n // 8, n % 8
                    b = g % NG
                    pt, Rp = ps[n % 2], Rps[n % 2]
                    pg.op("dve", lambda e: e.tensor_tensor(out=Pm[n % NP][:], in0=pt[:], in1=KKt[b][:, i, :],
                                                           op=ALU.mult), reads=[Rp, RBt[b]], writes=[RP[n % NP]])

                def s2(n):
                    pt2, Rp2 = ps[2 + n % 2], Rps[2 + n % 2]
                    pg.op("pe", lambda e: e.matmul(pt2[:], am[:], Pm[n % NP][:], start=True, stop=True),
                          reads=[R, RP[n % NP]], writes=[Rp2])

                def s3(n):
                    g, i = n // 8, n % 8
                    zb = g % 2
                    pt2, Rp2 = ps[2 + n % 2], Rps[2 + n % 2]
                    pg.op("act", lambda e: e.activation(out=Zst[zb][:, i, :], in_=pt2[:], func=AF.Copy),
                          reads=[Rp2], writes=[RZst[zb]])
                    if i == 7:
                        pg.dma("sp", Zd[g * 8:(g + 1) * 8, :, :].rearrange("f z c -> z f c"), Zst[zb][:],
                               reads=[RZst[zb]], writes=[Rd["Zd"]])

                skew(128, [s0, s1, s2, s3])
                pg.barrier()
            with contextlib.ExitStack() as st:
                vp = sb("h2_vp", [128, 4, T], BF16, st)
                x0 = sb("h2_x0", [128, 4, T], BF16, st)
                yh = sb("h2_y", [128, 4, T], BF16, st)
                rbr = sb("h2_rbr", [128, 64 * 64], BF16, st)
                rbi = sb("h2_rbi", [128, 64 * 64], BF16, st)
                hyb = sb("h2_hyb", [128, 4], F32, st)
                R = pg.res()
                Ry = pg.res()
                for c in range(4):
                    pg.dma("sp", vp[:, c, :], vpT[c * 128:(c + 1) * 128, :], reads=[Rd["vpT"]], writes=[R])
                    pg.dma("sp", x0[:, c, :], x0T[c * 128:(c + 1) * 128, :], reads=[Rd["x0T"]], writes=[R])
                pg.dma("sp", rbr[:], c_rbr, writes=[R])
                pg.dma("sp", rbi[:], c_rbi, writes=[R])
                pg.dma("sp", hyb[:], hyb_col, writes=[R])
                Zt = [sb("h2_Zt%d" % i, [128, 2, 8, 512], BF16, st) for i in range(2)]
                RZt = [pg.res() for _ in range(2)]
                tmp = [sb("h2_tmp%d" % i, [128, 8, 64], F32, st) for i in range(2)]
                Rtmp = [pg.res() for _ in range(2)]
                Zdv = Zd.rearrange("f (z t) c -> f z t c", z=2)
                nb = 0

                def load_zt(tg):
                    pg.dma("sp", Zt[tg % 2][:], Zdv[:, :, tg * 8:(tg + 1) * 8, :], reads=[Rd["Zd"]],
                           writes=[RZt[tg % 2]])

                load_zt(0)
                for tg in range(8):
                    b = tg % 2
                    if tg + 1 < 8:
                        load_zt(tg + 1)
                    for c in range(4):
                        tb = nb % 2
                        nb += 1
                        pt, Rp = next_ps()
                        for j in range(8):
                            t2i = tg * 8 + j
                            for z, rb in enumerate((rbr, rbi)):
                                pg.op("pe", lambda e: e.matmul(pt[:, j * 64:(j + 1) * 64],
                                                               Zt[b][:, z, j, c * 128:(c + 1) * 128],
                                                               rb[:, t2i * 64:(t2i + 1) * 64],
                                                               start=(z == 0), stop=(z == 1)),
                                      reads=[R, RZt[b]], writes=[Rp], inc=(j == 7 and z == 1))
                        vview = vp[:, c, :].rearrange("p (a b) -> p b a", b=64)[:, tg * 8:(tg + 1) * 8, :]
                        xview = x0[:, c, :].rearrange("p (a b) -> p b a", b=64)[:, tg * 8:(tg + 1) * 8, :]
                        yview = yh[:, c, :].rearrange("p (a b) -> p b a", b=64)[:, tg * 8:(tg + 1) * 8, :]
                        pview = pt[:].rearrange("p (j a) -> p j a", j=8)
                        pg.op("dve", lambda e: e.scalar_tensor_tensor(out=tmp[tb][:], in0=vview, scalar=hyb[:, c:c + 1],
                                                                      in1=pview, op0=ALU.mult, op1=ALU.add),
                              reads=[R, Rp], writes=[Rtmp[tb]])
                        pg.op("pool", lambda e: e.tensor_tensor(out=yview, in0=tmp[tb][:], in1=xview, op=ALU.mult),
                              reads=[Rtmp[tb], R], writes=[Ry])
                for c in range(4):
                    pg.dma("sp", yT[c * 128:(c + 1) * 128, :], yh[:, c, :], reads=[Ry], writes=[Rd["yT"]])
                pg.barrier()

        def phase_attn():
            with contextlib.ExitStack() as st:
                NBUF = 2
                qj = [sb("a_q%d" % i, [128, T], BF16, st) for i in range(NBUF)]
                kj = [sb("a_k%d" % i, [128, T], BF16, st) for i in range(NBUF)]
                kcj = [sb("a_kc%d" % i, [128, 256], BF16, st) for i in range(NBUF)]
                vE = [sb("a_vE%d" % i, [128, 32, 128], BF16, st) for i in range(NBUF)]
                vO = [sb("a_vO%d" % i, [128, 31, 128], BF16, st) for i in range(NBUF)]
                vC = [sb("a_vC%d" % i, [128, 2, 128], BF16, st) for i in range(NBUF)]
                t2t = [sb("a_t2%d" % i, [128, 2, 14 * 64], F32, st) for i in range(NBUF)]
                yn = [sb("a_yn%d" % i, [128, T], BF16, st) for i in range(NBUF)]
                Rin = [pg.res() for _ in range(NBUF)]
                Ryn = [pg.res() for _ in range(NBUF)]
                ones = sb("a_ones", [128, 64], BF16, st)
                Ro = pg.res()
                pg.op("dve", lambda e: e.memset(ones[:], 1.0), writes=[Ro])
                NS = 3
                Ein = [sb("a_E%d" % s_, [128, 2, 256], F32, st) for s_ in range(NS)]
                PT = [sb("a_P%d" % s_, [128, 2, 384], BF16, st) for s_ in range(NS)]
                rec = [sb("a_rec%d" % s_, [128, 64], F32, st) for s_ in range(NS)]
                RE = [pg.res() for _ in range(NS)]
                RPT = [pg.res() for _ in range(NS)]
                Rrec = [pg.res() for _ in range(NS)]
                Rss = [pg.res() for _ in range(3)]

                def load_pair(j):
                    pb = j % NBUF
                    Ri = Rin[pb]
                    pg.dma("sp", qj[pb][:], qT[j * 128:(j + 1) * 128, :], reads=[Rd["qT"]], writes=[Ri])
                    pg.dma("sp", kj[pb][:], kT[j * 128:(j + 1) * 128, :], reads=[Rd["kT"]], writes=[Ri])
                    pg.dma("sp", kcj[pb][:], kcT[j * 128:(j + 1) * 128, :], reads=[Rd["kcT"]], writes=[Ri])
                    cs = slice(j * 128, (j + 1) * 128)
                    pg.dma("sp", vE[pb][:], v_d[:, cs].rearrange("(i p) c -> p i c", p=128),
                           reads=[Rd["v_d"]], writes=[Ri])
                    pg.dma("sp", vO[pb][:], v_d[64:64 + 31 * 128, cs].rearrange("(i p) c -> p i c", p=128),
                           reads=[Rd["v_d"]], writes=[Ri])
                    pg.dma("sp", vC[pb][:], vc_d[:, cs].rearrange("(i p) c -> p i c", p=128),
                           reads=[Rd["vc_d"]], writes=[Ri])
                    for hh in range(2):
                        pg.dma("sp", t2t[pb][:, hh, :], rpb_t2[2 * j + hh], writes=[Ri])

                load_pair(0)
                for j in range(4):
                    pb = j % NBUF
                    Ri = Rin[pb]
                    if j + 1 < 4:
                        load_pair(j + 1)

                    def geo(r):
                        rs = min(max(r - 4, 0), 56)
                        return rs, rs - r + 7, 64 * rs

                    def s0(r):
                        rs, dr0, tb0 = geo(r)
                        s2_ = r % 3
                        for hh in range(2):
                            hp = slice(64 * hh, 64 * hh + 64)
                            pt = ps[2 * s2_ + hh]
                            for jj in range(6):
                                if jj < 4:
                                    lt = kj[pb][hp, tb0 + 128 * jj: tb0 + 128 * (jj + 1)]
                                else:
                                    lt = kcj[pb][hp, 128 * (jj - 4):128 * (jj - 3)]
                                pg.op("pe", lambda e: e.matmul(pt[:, jj * 64:(jj + 1) * 64], lt,
                                                               qj[pb][hp, 64 * r:64 * r + 64], start=True, stop=True),
                                      reads=[Ri], writes=[Rss[s2_]], inc=(hh == 1 and jj == 5))

                    def s1(r):
                        rs, dr0, tb0 = geo(r)
                        s2_ = r % 3
                        s3_ = r % NS
                        pS2 = psall[:, (2 * s2_) * 512:(2 * s2_ + 2) * 512].rearrange("p (h c) -> p h c", h=2)
                        tview = t2t[pb][:].rearrange("p h (m q) -> p h m q", q=64)[:, :, dr0:dr0 + 7:2, :]
                        pg.op("dve", lambda e: e.tensor_tensor(
                            out=Ein[s3_][:].rearrange("p h (m q) -> p h m q", q=64),
                            in0=pS2[:, :, 0:256].rearrange("p h (m q) -> p h m q", q=64), in1=tview, op=ALU.add),
                            reads=[Rss[s2_], Ri], writes=[RE[s3_]])

                    def s1b(r):
                        s2_ = r % 3
                        s3_ = r % NS
                        pS2 = psall[:, (2 * s2_) * 512:(2 * s2_ + 2) * 512].rearrange("p (h c) -> p h c", h=2)
                        pg.op("act", lambda e: e.activation(out=PT[s3_][:, :, 0:256], in_=Ein[s3_][:], func=AF.Exp),
                              reads=[RE[s3_]], writes=[RPT[s3_]])
                        pg.op("act", lambda e: e.activation(out=PT[s3_][:, :, 256:384], in_=pS2[:, :, 256:384],
                                                            func=AF.Exp), reads=[Rss[s2_]], writes=[RPT[s3_]])

                    def s2(r):
                        rs, dr0, tb0 = geo(r)
                        s3_ = r % NS
                        if rs % 2 == 0:
                            vsrc, i0_ = vE[pb], rs // 2
                        else:
                            vsrc, i0_ = vO[pb], (rs - 1) // 2
                        pO, RpO = ps[6 + r % 2], Rps[6 + r % 2]
                        for hh in range(2):
                            hp = slice(64 * hh, 64 * hh + 64)
                            for jj in range(6):
                                if jj < 4:
                                    vt = vsrc[:, i0_ + jj, 64 * hh:64 * hh + 64]
                                else:
                                    vt = vC[pb][:, jj - 4, 64 * hh:64 * hh + 64]
                                pg.op("pe", lambda e: e.matmul(pO[hp, 0:64], vt, PT[s3_][:, hh, jj * 64:(jj + 1) * 64],
                                                               start=(jj == 0), stop=(jj == 5)),
                                      reads=[Ri, RPT[s3_]], writes=[RpO], inc=False)
                            for jj in range(6):
                                pg.op("pe", lambda e: e.matmul(pO[hp, 64:128], ones[:],
                                                               PT[s3_][:, hh, jj * 64:(jj + 1) * 64],
                                                               start=(jj == 0), stop=(jj == 5)),
                                      reads=[Ro, RPT[s3_]], writes=[RpO], inc=(hh == 1 and jj == 5))

                    def s3(r):
                        s3_ = r % NS
                        pO, RpO = ps[6 + r % 2], Rps[6 + r % 2]
                        pg.op("dve", lambda e: e.reciprocal(out=rec[s3_][:], in_=pO[:, 64:128]), reads=[RpO],
                              writes=[Rrec[s3_]])
                        pg.op("dve", lambda e: e.tensor_tensor(out=yn[pb][:, 64 * r:64 * r + 64], in0=pO[:, 0:64],
                                                               in1=rec[s3_][:], op=ALU.mult),
                              reads=[RpO, Rrec[s3_]], writes=[Ryn[pb]])

                    skew(64, [s0, s1, s1b, s2, s3])
                    pg.dma("sp", yT[512 + j * 128:512 + (j + 1) * 128, :], yn[pb][:], reads=[Ryn[pb]],
                           writes=[Rd["yT"]])
                pg.barrier()

        def phase_outproj0():
            with contextlib.ExitStack() as st:
                yS = sb("o_y", [128, KC, T], BF16, st)
                wo = sb("o_w", [128, KC, D], BF16, st)
                R = pg.res()
                for kc in range(KC):
                    pg.dma("sp", yS[:, kc, :], yT[kc * 128:(kc + 1) * 128, :], reads=[Rd["yT"]], writes=[R])
                Rw = pg.res()
                for hf in range(2):
                    pg.dma("gq", wo[:, :, hf * 512:(hf + 1) * 512],
                           w_out.rearrange("(kc p) n -> p kc n", p=128)[:, :, hf * 512:(hf + 1) * 512], writes=[Rw])
                ru = RU(st, 0, 2, "o_ru")
                Rx = pg.res()
                hv = {}

                def s0(i):
                    ru.load(i, x, Rx)
                    halves = []
                    for hf in range(2):
                        pt, Rp = next_ps()
                        for kc in range(KC):
                            pg.op("pe", lambda e: e.matmul(pt[:], yS[:, kc, i * 128:(i + 1) * 128],
                                                           wo[:, kc, hf * 512:(hf + 1) * 512],
                                                           start=(kc == 0), stop=(kc == KC - 1)),
                                  reads=[R, Rw], writes=[Rp], inc=(kc == KC - 1))
                        halves.append((pt, Rp))
                    hv[i] = halves

                def s1(i):
                    halves = hv.pop(i)
                    ru.scale(i, [h[0][:] for h in halves], [h[1] for h in halves])

                def s2(i):
                    ru.finish(i, hbuf, Rd["hbuf"])

                skew(32, [s0, s1, s2])
                pg.barrier()

        def phase_ffn_up(l):
            hnT = H["hnT"]
            with contextlib.ExitStack() as st:
                fw = sb("f1_fw", [128, 44, 3], F32, st)
                fbv = sb("f1_fb", [128, 44], F32, st)
                Rc = pg.res()
                pg.dma("sp", fw[:], ffn_wdw_col[l], writes=[Rc])
                pg.dma("sp", fbv[:], ffn_bdw_col[l], writes=[Rc])
                NW = 3
                wg = [sb("f1_wg%d" % i, [128, KC, 128], BF16, st) for i in range(NW)]
                wu = [sb("f1_wu%d" % i, [128, KC, 128], BF16, st) for i in range(NW)]
                Rw = [pg.res() for _ in range(NW)]
                stg = [sb("f1_stg%d" % i, [128, T], BF16, st) for i in range(2)]
                Rstg = [pg.res() for _ in range(2)]
                NB = 3
                t0 = [sb("f1_t0%d" % i, [128, 512], F32, st) for i in range(2 * NB)]
                t1 = [sb("f1_t1%d" % i, [128, 512], F32, st) for i in range(2 * NB)]
                gc = [sb("f1_gc%d" % i, [128, 512], F32, st) for i in range(NB)]
                uc = [sb("f1_uc%d" % i, [128, 512], F32, st) for i in range(NB)]
                sg = [sb("f1_sg%d" % i, [128, 512], F32, st) for i in range(NB)]
                Rt = [pg.res() for _ in range(2 * NB)]
                Rgc = [pg.res() for _ in range(NB)]
                Ruc = [pg.res() for _ in range(NB)]
                Rsg = [pg.res() for _ in range(NB)]
                NT = len(TT_CONV)
                starts = [sum(TT_CONV[:k]) for k in range(NT)]

                def load_weights(j):
                    b = j % NW
                    load_w(wg[b], ffn_w_up[l], j * 128, 128, Rw[b])
                    load_w(wu[b], ffn_w_up[l], DFF + j * 128, 128, Rw[b])

                load_weights(0)
                pv = {}

                def s0(i):
                    j, k = i // NT, i % NT
                    b = j % NW
                    if k == 0 and j + 1 < NFF:
                        load_weights(j + 1)
                    n, s_ = TT_CONV[k], starts[k]
                    pg_, Rpg = next_ps()
                    for kc in range(KC):
                        pg.op("pe", lambda e: e.matmul(pg_[:, 0:n + 2], wg[b][:, kc, :], hnT[:, kc, s_:s_ + n + 2],
                                                       start=(kc == 0), stop=(kc == KC - 1)),
                              reads=[Rw[b], R_hnT], writes=[Rpg], inc=(kc == KC - 1))
                    pu_, Rpu = next_ps()
                    for kc in range(KC):
                        pg.op("pe", lambda e: e.matmul(pu_[:, 0:n + 2], wu[b][:, kc, :], hnT[:, kc, s_:s_ + n + 2],
                                                       start=(kc == 0), stop=(kc == KC - 1)),
                              reads=[Rw[b], R_hnT], writes=[Rpu], inc=(kc == KC - 1))
                    pv[i] = (pg_, Rpg, pu_, Rpu)

                def s1(i):
                    j, k = i // NT, i % NT
                    n = TT_CONV[k]
                    tb = i % NB
                    pg_, Rpg, pu_, Rpu = pv.pop(i)
                    conv3_evac(pg_, Rpg, n, fw[:, j, :], fbv[:, j:j + 1], Rc, t0[2 * tb], t1[2 * tb],
                               gc[tb][:, 0:n], Rt[2 * tb], Rgc[tb])
                    conv3_evac(pu_, Rpu, n, fw[:, NFF + j, :], fbv[:, NFF + j:NFF + j + 1], Rc, t0[2 * tb + 1],
                               t1[2 * tb + 1], uc[tb][:, 0:n], Rt[2 * tb + 1], Ruc[tb])

                def s2(i):
                    j, k = i // NT, i % NT
                    n, s_ = TT_CONV[k], starts[k]
                    tb = i % NB
                    sbuf_i = j % 2
                    pg.op("act", lambda e: e.activation(out=sg[tb][:, 0:n], in_=gc[tb][:, 0:n], func=AF.Silu),
                          reads=[Rgc[tb]], writes=[Rsg[tb]])
                    pg.op("pool", lambda e: e.tensor_tensor(out=stg[sbuf_i][:, s_:s_ + n], in0=sg[tb][:, 0:n],
                                                            in1=uc[tb][:, 0:n], op=ALU.mult),
                          reads=[Rsg[tb], Ruc[tb]], writes=[Rstg[sbuf_i]])
                    if k == NT - 1:
                        pg.dma("sp", actT[j * 128:(j + 1) * 128, :], stg[sbuf_i][:], reads=[Rstg[sbuf_i]],
                               writes=[Rd["actT"]])

                skew(NFF * NT, [s0, s1, s2])
                pg.barrier()

        def phase_ffn_down(l, hdst, Rdst):
            with contextlib.ExitStack() as st:
                wd = sb("f2_wd", [128, NFF, D], BF16, st)
                Rw = pg.res()
                wsrc = ffn_w_down[l].rearrange("(j p) n -> p j n", p=128)
                for j0 in range(0, NFF, 2):
                    pg.dma("gq", wd[:, j0:j0 + 2, :], wsrc[:, j0:j0 + 2, :], writes=[Rw])
                aS = [sb("f2_a%d" % i, [128, NFF, 512], BF16, st) for i in range(2)]
                Ra = [pg.res() for _ in range(2)]
                ru = RU(st, l, 5, "f2_ru")
                asrc = actT.rearrange("(j p) t -> p j t", p=128)
                hv = {}
                R_hload = pg.res()

                def load_slab(sl):
                    pg.dma("sp", aS[sl % 2][:], asrc[:, :, sl * 512:(sl + 1) * 512], reads=[Rd["actT"]],
                           writes=[Ra[sl % 2]])

                load_slab(0)

                def s0(i):
                    sl, sub = i // 4, i % 4
                    b = sl % 2
                    if sub == 0 and sl + 1 < 8:
                        load_slab(sl + 1)
                    ru.load(i, hbuf, R_hload)
                    halves = []
                    for hf in range(2):
                        pt, Rp = next_ps()
                        for j in range(NFF):
                            pg.op("pe", lambda e: e.matmul(pt[:], aS[b][:, j, sub * 128:(sub + 1) * 128],
                                                           wd[:, j, hf * 512:(hf + 1) * 512],
                                                           start=(j == 0), stop=(j == NFF - 1)),
                                  reads=[Ra[b], Rw], writes=[Rp], inc=(j == NFF - 1))
                        halves.append((pt, Rp))
                    hv[i] = halves

                def s1(i):
                    halves = hv.pop(i)
                    ru.scale(i, [h[0][:] for h in halves], [h[1] for h in halves])

                def s2(i):
                    ru.finish(i, hdst, Rdst)

                skew(32, [s0, s1, s2])
                pg.barrier()

        def phase_conformer_a():
            hnT = H["hnT"]
            with contextlib.ExitStack() as st:
                cb1 = sb("c_b1", [128, 16], F32, st)
                cw = sb("c_w", [128, 8, 31], F32, st)
                cbd = sb("c_bd", [128, 8], F32, st)
                Rc = pg.res()
                pg.dma("sp", cb1[:], cf_b1_col, writes=[Rc])
                pg.dma("sp", cw[:], cf_wdw_col, writes=[Rc])
                pg.dma("sp", cbd[:], cf_bdw_col, writes=[Rc])
                wa = [sb("c_wa%d" % i, [128, KC, 128], BF16, st) for i in range(2)]
                wgt = [sb("c_wg%d" % i, [128, KC, 128], BF16, st) for i in range(2)]
                Rw = [pg.res() for _ in range(2)]
                uT = [sb("c_uT%d" % i, [128, T + 30], BF16, st) for i in range(2)]
                RuT = [pg.res() for _ in range(2)]
                dg = [sb("c_dg%d" % i, [128, 31, 128], BF16, st) for i in range(2)]
                Rdg = [pg.res() for _ in range(2)]
                sig = [sb("c_sig%d" % i, [128, 512], F32, st) for i in range(2)]
                Rsig = [pg.res() for _ in range(2)]
                u2s = [sb("c_u2s%d" % i, [128, T], F32, st) for i in range(2)]
                Ru2 = [pg.res() for _ in range(2)]
                for b in range(2):
                    pg.op("dve", lambda e: e.memset(uT[b][:, 0:15], 0.0), writes=[RuT[b]])
                    pg.op("dve", lambda e: e.memset(uT[b][:, T + 15:T + 30], 0.0), writes=[RuT[b]])
                nblk = 0

                def prep(j):
                    b = j % 2
                    load_w(wa[b], cf_w_pw1, j * 128, 128, Rw[b])
                    load_w(wgt[b], cf_w_pw1, D + j * 128, 128, Rw[b])
                    for tap in range(31):
                        pg.op("pool", lambda e: e.tensor_scalar(out=dg[b][:, tap, :], in0=ident[:],
                                                                scalar1=cw[:, j, tap:tap + 1], scalar2=1.0,
                                                                op0=ALU.mult, op1=ALU.mult),
                              reads=[R_const, Rc], writes=[Rdg[b]])

                prep(0)
                for j in range(8):
                    b = j % 2
                    if j + 1 < 8:
                        prep(j + 1)
                    for ti in range(8):
                        tb = nblk % 2
                        nblk += 1
                        cs = slice(1 + ti * 512, 1 + (ti + 1) * 512)
                        pa, Rpa = next_ps()
                        for kc in range(KC):
                            pg.op("pe", lambda e: e.matmul(pa[:], wa[b][:, kc, :], hnT[:, kc, cs],
                                                           start=(kc == 0), stop=(kc == KC - 1)),
                                  reads=[Rw[b], R_hnT], writes=[Rpa], inc=(kc == KC - 1))
                        pgt, Rpg = next_ps()
                        for kc in range(KC):
                            pg.op("pe", lambda e: e.matmul(pgt[:], wgt[b][:, kc, :], hnT[:, kc, cs],
                                                           start=(kc == 0), stop=(kc == KC - 1)),
                                  reads=[Rw[b], R_hnT], writes=[Rpg], inc=(kc == KC - 1))
                        pg.op("act", lambda e: e.activation(out=sig[tb][:], in_=pgt[:], func=AF.Sigmoid,
                                                            bias=cb1[:, 8 + j:9 + j]),
                              reads=[Rpg, Rc], writes=[Rsig[tb]])
                        pg.op("dve", lambda e: e.scalar_tensor_tensor(
                            out=uT[b][:, 15 + ti * 512:15 + (ti + 1) * 512], in0=pa[:], scalar=cb1[:, j:j + 1],
                            in1=sig[tb][:], op0=ALU.add, op1=ALU.mult),
                            reads=[Rpa, Rc, Rsig[tb]], writes=[RuT[b]])
                    for ti in range(8):
                        pt, Rp = next_ps()
                        for tap in range(31):
                            pg.op("pe", lambda e: e.matmul(pt[:], dg[b][:, tap, :],
                                                           uT[b][:, ti * 512 + tap: ti * 512 + tap + 512],
                                                           start=(tap == 0), stop=(tap == 30)),
                                  reads=[Rdg[b], RuT[b]], writes=[Rp], inc=(tap == 30))
                        pg.op("act", lambda e: e.activation(out=u2s[b][:, ti * 512:(ti + 1) * 512], in_=pt[:],
                                                            func=AF.Identity, bias=cbd[:, j:j + 1]),
                              reads=[Rp, Rc], writes=[Ru2[b]])
                    pg.dma("sp", u2T[j * 128:(j + 1) * 128, :], u2s[b][:], reads=[Ru2[b]], writes=[Rd["u2T"]])
                pg.barrier()

        def phase_conformer_b():
            with contextlib.ExitStack() as st:
                lng = sb("c2_g", [128, 8], F32, st)
                lnb = sb("c2_b", [128, 8], F32, st)
                onesf = sb("c2_ones", [128, 128], F32, st)
                bp2 = sb("c2_bp2", [128, D], F32, st)
                wp2 = sb("c2_w", [128, KC, D], BF16, st)
                Rc = pg.res()
                pg.dma("sp", lng[:], cf_lng_col, writes=[Rc])
                pg.dma("sp", lnb[:], cf_lnb_col, writes=[Rc])
                pg.dma("sp", bp2[:], cf_b_pw2.partition_broadcast(128)[:, 0, :], writes=[Rc])
                pg.op("dve", lambda e: e.memset(onesf[:], 1.0 / D), writes=[Rc])
                for hf in range(2):
                    pg.dma("gq", wp2[:, :, hf * 512:(hf + 1) * 512],
                           cf_w_pw2.rearrange("(kc p) n -> p kc n", p=128)[:, :, hf * 512:(hf + 1) * 512],
                           writes=[Rc])
                u2 = [sb("c2_u2%d" % i, [128, KC, 512], F32, st) for i in range(2)]
                Ru2 = [pg.res() for _ in range(2)]
                sq = sb("c2_sq", [128, KC, 512], F32, st)
                Rsq = pg.res()
                mean = sb("c2_mean", [128, 512], F32, st)
                m2 = sb("c2_m2", [128, 512], F32, st)
                var = sb("c2_var", [128, 512], F32, st)
                rstd = sb("c2_rstd", [128, 512], F32, st)
                Rst = pg.res()
                d1 = [sb("c2_d1%d" % i, [128, 512], F32, st) for i in range(2)]
                d2 = [sb("c2_d2%d" % i, [128, 512], F32, st) for i in range(2)]
                Rd1 = [pg.res() for _ in range(2)]
                Rd2 = [pg.res() for _ in range(2)]
                u3 = [sb("c2_u3%d" % i, [128, KC, 512], BF16, st) for i in range(2)]
                Ru3 = [pg.res() for _ in range(2)]
                ysb = [sb("c2_y%d" % i, [128, D], F32, st) for i in range(2)]
                Ry = [pg.res() for _ in range(2)]
                ru = RU(st, 1, 2, "c2_ru")
                R_hload = pg.res()
                u2src = u2T.rearrange("(kc p) t -> p kc t", p=128)
                nd = 0
                ny = 0
                def load_u2(ti):
                    pg.dma("sp", u2[ti % 2][:], u2src[:, :, ti * 512:(ti + 1) * 512], reads=[Rd["u2T"]],
                           writes=[Ru2[ti % 2]])

                load_u2(0)
                cnt = {"nd": 0, "ny": 0}

                def sA(ti):
                    b = ti % 2
                    if ti + 1 < 8:
                        load_u2(ti + 1)
                    pg.op("act", lambda e: e.activation(out=sq[:], in_=u2[b][:], func=AF.Square),
                          reads=[Ru2[b]], writes=[Rsq])
                    pm, Rpm = next_ps()
                    for kc in range(KC):
                        pg.op("pe", lambda e: e.matmul(pm[:], onesf[:], u2[b][:, kc, :], start=(kc == 0),
                                                       stop=(kc == KC - 1)),
                              reads=[Rc, Ru2[b]], writes=[Rpm], inc=(kc == KC - 1))
                    pq, Rpq = next_ps()
                    for kc in range(KC):
                        pg.op("pe", lambda e: e.matmul(pq[:], onesf[:], sq[:, kc, :], start=(kc == 0),
                                                       stop=(kc == KC - 1)),
                              reads=[Rc, Rsq], writes=[Rpq], inc=(kc == KC - 1))
                    pg.op("act", lambda e: e.activation(out=mean[:], in_=pm[:], func=AF.Copy), reads=[Rpm],
                          writes=[Rst])
                    pg.op("dve", lambda e: e.tensor_tensor(out=m2[:], in0=mean[:], in1=mean[:], op=ALU.mult),
                          reads=[Rst], writes=[Rst])
                    pg.op("dve", lambda e: e.tensor_tensor(out=var[:], in0=pq[:], in1=m2[:], op=ALU.subtract),
                          reads=[Rpq, Rst], writes=[Rst])
                    pg.op("act", lambda e: e.activation(out=var[:], in_=var[:], func=AF.Ln, bias=epsc[:, 0:1]),
                          reads=[Rst, R_const], writes=[Rst])
                    pg.op("act", lambda e: e.activation(out=rstd[:], in_=var[:], func=AF.Exp, scale=-0.5),
                          reads=[Rst], writes=[Rst])
                    for kc in range(KC):
                        db = cnt["nd"] % 2
                        cnt["nd"] += 1
                        pg.op("dve", lambda e: e.tensor_tensor(out=d1[db][:], in0=u2[b][:, kc, :], in1=mean[:],
                                                               op=ALU.subtract),
                              reads=[Ru2[b], Rst], writes=[Rd1[db]])
                        pg.op("pool", lambda e: e.tensor_tensor(out=d2[db][:], in0=d1[db][:], in1=rstd[:],
                                                                op=ALU.mult),
                              reads=[Rd1[db], Rst], writes=[Rd2[db]])
                        pg.op("act", lambda e: e.activation(out=u3[b][:, kc, :], in_=d2[db][:], func=AF.Silu,
                                                            bias=lnb[:, kc:kc + 1], scale=lng[:, kc:kc + 1]),
                              reads=[Rd2[db], Rc], writes=[Ru3[b]])

                def sB(ti):
                    b = ti % 2
                    for sub in range(4):
                        i = ti * 4 + sub
                        yb = cnt["ny"] % 2
                        cnt["ny"] += 1
                        for hf in range(2):
                            pt, Rp = next_ps()
                            for kc in range(KC):
                                pg.op("pe", lambda e: e.matmul(pt[:], u3[b][:, kc, sub * 128:(sub + 1) * 128],
                                                               wp2[:, kc, hf * 512:(hf + 1) * 512],
                                                               start=(kc == 0), stop=(kc == KC - 1)),
                                      reads=[Ru3[b], Rc], writes=[Rp], inc=(kc == KC - 1))
                            pg.op("dve", lambda e: e.tensor_tensor(out=ysb[yb][:, hf * 512:(hf + 1) * 512], in0=pt[:],
                                                                   in1=bp2[:, hf * 512:(hf + 1) * 512], op=ALU.add),
                                  reads=[Rp, Rc], writes=[Ry[yb]])
                        ru.load(i, hbuf, R_hload)
                        ru.scale(i, [ysb[yb][:, 0:512], ysb[yb][:, 512:1024]], [Ry[yb], Ry[yb]])
                        ru.finish(i, hbuf, Rd["hbuf"])

                skew(8, [sA, sB], reverse=False)
                pg.barrier()

        hn_stack = contextlib.ExitStack()

        def hn_on():
            hn_open(hn_stack)

        def hn_off():
            hn_stack.close()

        steps = [
            ("mod0", lambda: phase_mod(0)),
            ("filters", phase_filters),
            ("hn_on", hn_on),
            ("pn0", lambda: phase_prenorm(x, pg.res(), T, 0, 0, 1, H["hnT"], R_hnT, 1)),
            ("proj0", phase_proj0),
            ("hn_off", hn_off),
            ("hyena", phase_hyena),
            ("attn", phase_attn),
            ("outproj0", phase_outproj0),
            ("hn_on", hn_on),
            ("pn0f", lambda: phase_prenorm(hbuf, Rd["hbuf"], T, 0, 3, 4, H["hnT"], R_hnT, 1)),
            ("ffn0u", lambda: phase_ffn_up(0)),
            ("hn_off", hn_off),
            ("ffn0d", lambda: phase_ffn_down(0, hbuf, Rd["hbuf"])),
            ("mod1", lambda: phase_mod(1)),
            ("hn_on", hn_on),
            ("pn1", lambda: phase_prenorm(hbuf, Rd["hbuf"], T, 1, 0, 1, H["hnT"], R_hnT, 1)),
            ("confa", phase_conformer_a),
            ("hn_off", hn_off),
            ("confb", phase_conformer_b),
            ("hn_on", hn_on),
            ("pn1f", lambda: phase_prenorm(hbuf, Rd["hbuf"], T, 1, 3, 4, H["hnT"], R_hnT, 1)),
            ("ffn1u", lambda: phase_ffn_up(1)),
            ("hn_off", hn_off),
            ("ffn1d", lambda: phase_ffn_down(1, out, Rd["out"])),
        ]
        skip = set(skip_steps or ())
        for name, fn in steps:
            if name in skip:
                continue
            fn()
            if stop is not None and name == stop:
                break
        pg.barrier()
        hn_stack.close()
    return nc, dbg_names


def make_in_maps(inputs):
    f32 = lambda a: np.ascontiguousarray(np.asarray(a, np.float32))
    consts = make_consts()
    shared = {
        "w_mod": f32(inputs["w_mod"]), "b_mod": f32(inputs["b_mod"]),
        "gvec": f32(np.stack([inputs["g_mix_pre"], inputs["g_mix_post"], inputs["g_ffn_pre"],
                              inputs["g_ffn_post"]], axis=1)),
        "w_in": f32(inputs["w_in"][0]), "w_out": f32(inputs["w_out"][0]),
        "hsw_col": col(inputs["hy_short_w"][0], 12), "hsb_col": col(inputs["hy_short_b"][0], 12),
        "hf_w1": f32(inputs["hy_f_w1"][0]), "hf_w2": f32(inputs["hy_f_w2"][0]), "hf_w3": f32(inputs["hy_f_w3"][0]),
        "hf_w4": f32(inputs["hy_f_w4"][0]),
        "hf_b_col": f32(np.stack([inputs["hy_f_b1"][0], inputs["hy_f_b2"][0], inputs["hy_f_b3"][0]], axis=1)),
        "hf_freq_col": f32(np.asarray(inputs["hy_f_freq"][0]).T),
        "hyb_col": col(inputs["hy_bias"][0], 4),
        "rpb_t2": make_rpb_table(np.asarray(inputs["na_rpb"][0], np.float32)),
        "cf_w_pw1": f32(inputs["cf_w_pw1"][0]), "cf_b1_col": col(inputs["cf_b_pw1"][0], 16),
        "cf_wdw_col": col(inputs["cf_w_dw"][0], 8), "cf_bdw_col": col(inputs["cf_b_dw"][0], 8),
        "cf_lng_col": col(inputs["cf_ln_g"][0], 8), "cf_lnb_col": col(inputs["cf_ln_b"][0], 8),
        "cf_w_pw2": f32(inputs["cf_w_pw2"][0]), "cf_b_pw2": f32(np.asarray(inputs["cf_b_pw2"][0])[None, :]),
        "ffn_w_up": f32(inputs["ffn_w_up"]),
        "ffn_wdw_col": np.stack([col(inputs["ffn_w_dw"][l], 44) for l in range(2)]),
        "ffn_bdw_col": np.stack([col(inputs["ffn_b_dw"][l], 44) for l in range(2)]),
        "ffn_w_down": f32(inputs["ffn_w_down"]),
    }
    shared.update(consts)
    maps = []
    for b in range(NCORES):
        m = dict(shared)
        m["x"] = f32(inputs["x"][b])
        m["ctx"] = f32(inputs["ctx"][b])
        cc = np.stack([np.asarray(inputs["c"][b], np.float32), np.asarray(inputs["c_ctx"], np.float32)], 0)
        m["ccol"] = np.ascontiguousarray(np.transpose(cc.reshape(2, 8, 128), (2, 0, 1)))
        maps.append(m)
    return maps


def kernel(**inputs):
    inputs = {k: np.asarray(v) for k, v in inputs.items()}
    nc, _ = build()
    maps = make_in_maps(inputs)
    for m in maps:
        for k, (shp, dt) in INPUT_SPECS.items():
            assert m[k].shape == shp and m[k].dtype == dt, (k, m[k].shape, shp, m[k].dtype, dt)
    res = run_bass_kernel_spmd(nc, maps, core_ids=list(range(NCORES)))
    outs = [np.asarray(res.results[b]["out"], np.float32) for b in range(NCORES)]
    return np.stack(outs, 0)
```
